# Optimizing a Trainium2 kernel written in Bass

```python
import math
import jax
import jax.numpy as jnp
from jax import lax
import numpy as np

D_MODEL = 2048
BATCH = 4
SEQ = 4096
DEPTH = 2

N_EVEN = (DEPTH + 1) // 2
N_ODD = DEPTH // 2
D_FF = 5632
EPS = 1e-6
CONV_CH = D_MODEL // 2
CONV_WIDTH = 31
DN_HEADS = 8
DN_DK = 128
DN_DV = 128
DN_QK = DN_HEADS * DN_DK
DN_WIDTH = DN_HEADS * DN_DV
SHORT_CONV = 3
CHUNK = 64
EVEN_IN = 2 * CONV_CH + 2 * DN_QK + 2 * DN_WIDTH + 4 * DN_HEADS
DA_HEADS = 8
DA_DH = D_MODEL // (2 * DA_HEADS)
ROPE_THETA = 500000.0
ROPE_DIMS = DA_DH // 4
Q_BLOCK = 128

kernel_name = "hybrid_conformer_deltanet_diffattn_encoder"


def rmsnorm(x, g):
    xf = x.astype(jnp.float32)
    y = xf * lax.rsqrt(jnp.mean(xf * xf, axis=-1, keepdims=True) + EPS)
    return (y * g.astype(jnp.float32)).astype(x.dtype)


def layernorm(x, g, b):
    xf = x.astype(jnp.float32)
    mu = jnp.mean(xf, axis=-1, keepdims=True)
    xc = xf - mu
    y = xc * lax.rsqrt(jnp.mean(xc * xc, axis=-1, keepdims=True) + EPS)
    return (y * g.astype(jnp.float32) + b.astype(jnp.float32)).astype(x.dtype)


def l2norm(x):
    xf = x.astype(jnp.float32)
    return xf * lax.rsqrt(jnp.sum(xf * xf, axis=-1, keepdims=True) + EPS)


def swiglu(x, wg, wu, wd):
    return (jax.nn.silu(x @ wg) * (x @ wu)) @ wd


def depthwise_conv(x, w):
    k = w.shape[0]
    return lax.conv_general_dilated(
        x, w[:, None, :].astype(x.dtype), window_strides=(1,),
        padding=[(k // 2, k // 2)], dimension_numbers=("NWC", "WIO", "NWC"),
        feature_group_count=x.shape[-1])


def gated_delta_rule(q, k, v, g, beta):
    b, h, s, dk = q.shape
    dv = v.shape[-1]
    n = s // CHUNK
    q = q.reshape(b, h, n, CHUNK, dk)
    k = k.reshape(b, h, n, CHUNK, dk)
    v = v.reshape(b, h, n, CHUNK, dv)
    beta = beta.reshape(b, h, n, CHUNK)
    gc = jnp.cumsum(g.reshape(b, h, n, CHUNK), axis=-1)
    idx = jnp.arange(CHUNK)
    incl = idx[:, None] >= idx[None, :]
    strict = idx[:, None] > idx[None, :]
    decay = jnp.exp(jnp.where(incl, gc[..., :, None] - gc[..., None, :], -jnp.inf))
    kb = k * beta[..., None]
    vb = v * beta[..., None]
    lmat = jnp.where(strict, jnp.einsum("bhnid,bhnjd->bhnij", kb, k) * decay, 0.0)
    eye = jnp.eye(CHUNK, dtype=jnp.float32)
    tmat = lax.linalg.triangular_solve(lmat + eye, jnp.broadcast_to(eye, lmat.shape),
                                       left_side=True, lower=True, unit_diagonal=True)
    gexp = jnp.exp(gc)
    u = tmat @ vb
    w = tmat @ (kb * gexp[..., None])
    a_intra = jnp.einsum("bhnid,bhnjd->bhnij", q, k) * decay
    q_dec = q * gexp[..., None]
    g_last = gc[..., -1]
    k_dec = k * jnp.exp(g_last[..., None] - gc)[..., None]
    xs = tuple(jnp.moveaxis(t, 2, 0) for t in (u, w, q_dec, k_dec, a_intra, jnp.exp(g_last)))

    def step(state, inp):
        u_i, w_i, qd_i, kd_i, a_i, gl_i = inp
        v_new = u_i - jnp.einsum("bhcd,bhde->bhce", w_i, state)
        o_i = jnp.einsum("bhcd,bhde->bhce", qd_i, state) + jnp.einsum("bhij,bhje->bhie", a_i, v_new)
        state = state * gl_i[..., None, None] + jnp.einsum("bhcd,bhce->bhde", kd_i, v_new)
        return state, o_i

    state0 = jnp.zeros((b, h, dk, dv), jnp.float32)
    _, o = lax.scan(step, state0, xs)
    return jnp.moveaxis(o, 0, 2).reshape(b, h, s, dv)


def even_mixer(hx, w_in, conv_w, conv_b, ln_g, ln_b, short_w, a_log, dt_bias, onorm_g, w_out):
    b, s, _ = hx.shape
    p = hx @ w_in
    i1 = CONV_CH
    i2 = 2 * CONV_CH
    i3 = i2 + 2 * DN_QK + DN_WIDTH
    i4 = i3 + DN_WIDTH
    i5 = i4 + 2 * DN_HEADS
    glu_v, glu_g, qkv, z, beta_raw, alpha_raw = jnp.split(p, [i1, i2, i3, i4, i5], axis=-1)
    a = glu_v * jax.nn.sigmoid(glu_g)
    a = depthwise_conv(a, conv_w) + conv_b.astype(a.dtype)
    a = jax.nn.silu(layernorm(a, ln_g, ln_b))
    qkv = jax.nn.silu(depthwise_conv(qkv, short_w))
    q, k, v = jnp.split(qkv, [DN_QK, 2 * DN_QK], axis=-1)
    q = l2norm(q.reshape(b, s, DN_HEADS, DN_DK)) * (DN_DK ** -0.5)
    k = l2norm(k.reshape(b, s, DN_HEADS, DN_DK))
    v = v.reshape(b, s, DN_HEADS, DN_DV).astype(jnp.float32)
    beta = jax.nn.sigmoid(beta_raw.astype(jnp.float32)).reshape(b, s, 2, DN_HEADS)
    g = -jnp.exp(a_log.astype(jnp.float32)) * jax.nn.softplus(
        alpha_raw.astype(jnp.float32).reshape(b, s, 2, DN_HEADS) + dt_bias.astype(jnp.float32))
    qh = jnp.transpose(q, (0, 2, 1, 3))
    kh = jnp.transpose(k, (0, 2, 1, 3))
    vh = jnp.transpose(v, (0, 2, 1, 3))
    g_f = jnp.transpose(g[:, :, 0], (0, 2, 1))
    g_b = jnp.transpose(g[:, :, 1], (0, 2, 1))
    b_f = jnp.transpose(beta[:, :, 0], (0, 2, 1))
    b_b = jnp.transpose(beta[:, :, 1], (0, 2, 1))
    o_fwd = gated_delta_rule(qh, kh, vh, g_f, b_f)
    o_bwd = jnp.flip(gated_delta_rule(jnp.flip(qh, 2), jnp.flip(kh, 2), jnp.flip(vh, 2),
                                      jnp.flip(g_b, 2), jnp.flip(b_b, 2)), 2)
    o = jnp.transpose(o_fwd + o_bwd, (0, 2, 1, 3))
    o = rmsnorm(o, onorm_g) * jax.nn.silu(z.reshape(b, s, DN_HEADS, DN_DV).astype(jnp.float32))
    o = o.reshape(b, s, DN_WIDTH).astype(hx.dtype)
    return jnp.concatenate([a, o], axis=-1) @ w_out


def partial_rotary(x, cos, sin):
    half = ROPE_DIMS // 2
    x1 = x[..., :half]
    x2 = x[..., half:ROPE_DIMS]
    return jnp.concatenate([x1 * cos - x2 * sin, x2 * cos + x1 * sin, x[..., ROPE_DIMS:]], axis=-1)


def odd_mixer(hx, positions, w_qkv, lq1, lk1, lq2, lk2, subln_g, w_o, lambda_init):
    b, s, _ = hx.shape
    q, k, v = jnp.split(hx @ w_qkv, [D_MODEL, 2 * D_MODEL], axis=-1)
    q = q.reshape(b, s, DA_HEADS, 2, DA_DH)
    k = k.reshape(b, s, DA_HEADS, 2, DA_DH)
    v = v.reshape(b, s, DA_HEADS, 2 * DA_DH)
    inv_freq = ROPE_THETA ** (-jnp.arange(0, ROPE_DIMS, 2, dtype=jnp.float32) / ROPE_DIMS)
    ang = positions.astype(jnp.float32)[..., None] * inv_freq
    cos = jnp.cos(ang)[:, :, None, None, :].astype(hx.dtype)
    sin = jnp.sin(ang)[:, :, None, None, :].astype(hx.dtype)
    q = partial_rotary(q, cos, sin) * (DA_DH ** -0.5)
    k = partial_rotary(k, cos, sin)
    nb = s // Q_BLOCK
    qb = jnp.transpose(q.reshape(b, nb, Q_BLOCK, DA_HEADS, 2, DA_DH), (1, 0, 3, 4, 2, 5))
    kt = jnp.transpose(k, (0, 2, 3, 1, 4))
    vt = jnp.transpose(v, (0, 2, 1, 3))
    lam = (jnp.exp(jnp.sum(lq1.astype(jnp.float32) * lk1.astype(jnp.float32)))
           - jnp.exp(jnp.sum(lq2.astype(jnp.float32) * lk2.astype(jnp.float32))) + lambda_init)

    def attend(q_blk):
        sc = jnp.einsum("bhmqd,bhmkd->bhmqk", q_blk, kt).astype(jnp.float32)
        pr = jax.nn.softmax(sc, axis=-1)
        wgt = pr[:, :, 0] - lam * pr[:, :, 1]
        return jnp.einsum("bhqk,bhke->bhqe", wgt.astype(vt.dtype), vt)

    o = lax.map(attend, qb)
    o = jnp.transpose(o, (1, 0, 3, 2, 4)).reshape(b, s, DA_HEADS, 2 * DA_DH)
    o = rmsnorm(o, subln_g) * (1.0 - lambda_init)
    return o.reshape(b, s, D_MODEL) @ w_o


def setup_inputs(seed: int = 0) -> dict:
    key = jax.random.key(seed)
    ks = iter(jax.random.split(key, 48))

    def nrm(shape, scale):
        return jax.random.normal(next(ks), shape, jnp.float32) * scale

    def gain(shape):
        return 1.0 + nrm(shape, 0.02)

    x = nrm((BATCH, SEQ, D_MODEL), 1.0)
    positions = jnp.broadcast_to(jnp.arange(SEQ, dtype=jnp.int32), (BATCH, SEQ))
    norm_ffn1 = gain((DEPTH, D_MODEL))
    ffn1_wg = nrm((DEPTH, D_MODEL, D_FF), D_MODEL ** -0.5)
    ffn1_wu = nrm((DEPTH, D_MODEL, D_FF), D_MODEL ** -0.5)
    ffn1_wd = nrm((DEPTH, D_FF, D_MODEL), D_FF ** -0.5)
    norm_mix = gain((DEPTH, D_MODEL))
    norm_ffn2 = gain((DEPTH, D_MODEL))
    ffn2_wg = nrm((DEPTH, D_MODEL, D_FF), D_MODEL ** -0.5)
    ffn2_wu = nrm((DEPTH, D_MODEL, D_FF), D_MODEL ** -0.5)
    ffn2_wd = nrm((DEPTH, D_FF, D_MODEL), D_FF ** -0.5)
    ev_w_in = nrm((N_EVEN, D_MODEL, EVEN_IN), D_MODEL ** -0.5)
    ev_conv_w = nrm((N_EVEN, CONV_WIDTH, CONV_CH), CONV_WIDTH ** -0.5)
    ev_conv_b = nrm((N_EVEN, CONV_CH), 0.02)
    ev_ln_g = gain((N_EVEN, CONV_CH))
    ev_ln_b = nrm((N_EVEN, CONV_CH), 0.02)
    ev_short_w = nrm((N_EVEN, SHORT_CONV, 2 * DN_QK + DN_WIDTH), SHORT_CONV ** -0.5)
    ev_a_log = jnp.log(jax.random.uniform(next(ks), (N_EVEN, 2, DN_HEADS), jnp.float32, 1.0, 16.0))
    dt = jnp.exp(jax.random.uniform(next(ks), (N_EVEN, 2, DN_HEADS), jnp.float32,
                                    math.log(1e-3), math.log(1e-1)))
    ev_dt_bias = dt + jnp.log(-jnp.expm1(-dt))
    ev_onorm_g = gain((N_EVEN, DN_DV))
    ev_w_out = nrm((N_EVEN, CONV_CH + DN_WIDTH, D_MODEL), (CONV_CH + DN_WIDTH) ** -0.5)
    od_w_qkv = nrm((N_ODD, D_MODEL, 3 * D_MODEL), D_MODEL ** -0.5)
    od_lq1 = nrm((N_ODD, DA_DH), 0.1)
    od_lk1 = nrm((N_ODD, DA_DH), 0.1)
    od_lq2 = nrm((N_ODD, DA_DH), 0.1)
    od_lk2 = nrm((N_ODD, DA_DH), 0.1)
    od_subln_g = gain((N_ODD, 2 * DA_DH))
    od_w_o = nrm((N_ODD, D_MODEL, D_MODEL), D_MODEL ** -0.5)
    final_norm = gain((D_MODEL,))
    return {"x": x, "positions": positions,
            "norm_ffn1": norm_ffn1, "ffn1_wg": ffn1_wg, "ffn1_wu": ffn1_wu, "ffn1_wd": ffn1_wd,
            "norm_mix": norm_mix,
            "norm_ffn2": norm_ffn2, "ffn2_wg": ffn2_wg, "ffn2_wu": ffn2_wu, "ffn2_wd": ffn2_wd,
            "ev_w_in": ev_w_in, "ev_conv_w": ev_conv_w, "ev_conv_b": ev_conv_b,
            "ev_ln_g": ev_ln_g, "ev_ln_b": ev_ln_b, "ev_short_w": ev_short_w,
            "ev_a_log": ev_a_log, "ev_dt_bias": ev_dt_bias, "ev_onorm_g": ev_onorm_g, "ev_w_out": ev_w_out,
            "od_w_qkv": od_w_qkv, "od_lq1": od_lq1, "od_lk1": od_lk1, "od_lq2": od_lq2, "od_lk2": od_lk2,
            "od_subln_g": od_subln_g, "od_w_o": od_w_o,
            "final_norm": final_norm}


def reference(x, positions, norm_ffn1, ffn1_wg, ffn1_wu, ffn1_wd, norm_mix,
              norm_ffn2, ffn2_wg, ffn2_wu, ffn2_wd,
              ev_w_in, ev_conv_w, ev_conv_b, ev_ln_g, ev_ln_b, ev_short_w,
              ev_a_log, ev_dt_bias, ev_onorm_g, ev_w_out,
              od_w_qkv, od_lq1, od_lk1, od_lq2, od_lk2, od_subln_g, od_w_o,
              final_norm):
    for i in range(DEPTH):
        x = x + 0.5 * swiglu(rmsnorm(x, norm_ffn1[i]), ffn1_wg[i], ffn1_wu[i], ffn1_wd[i])
        hx = rmsnorm(x, norm_mix[i])
        j = i // 2
        if i % 2 == 0:
            x = x + even_mixer(hx, ev_w_in[j], ev_conv_w[j], ev_conv_b[j], ev_ln_g[j], ev_ln_b[j],
                               ev_short_w[j], ev_a_log[j], ev_dt_bias[j], ev_onorm_g[j], ev_w_out[j])
        else:
            lambda_init = 0.8 - 0.6 * math.exp(-0.3 * i)
            x = x + odd_mixer(hx, positions, od_w_qkv[j], od_lq1[j], od_lk1[j], od_lq2[j], od_lk2[j],
                              od_subln_g[j], od_w_o[j], lambda_init)
        x = x + 0.5 * swiglu(rmsnorm(x, norm_ffn2[i]), ffn2_wg[i], ffn2_wu[i], ffn2_wd[i])
    return rmsnorm(x, final_norm)
```

```python
import numpy as np
from contextlib import ExitStack
import concourse.bass as bass
import concourse.mybir as mybir
from concourse.bass_utils import run_bass_kernel_spmd

F32 = mybir.dt.float32
BF16 = mybir.dt.bfloat16
I32 = mybir.dt.int32
AF = mybir.ActivationFunctionType
ALU = mybir.AluOpType
AX = mybir.AxisListType

NCORES = 8
D_MODEL = 2048
D_FF = 5632
SEQ = 4096
BATCH = 4
EPS = 1e-6


class Buf:
    __slots__ = ("name", "w", "r")

    def __init__(self, name):
        self.name = name
        self.w = None
        self.r = []


class _Op:
    __slots__ = ("eng", "fn", "reads", "writes", "dma_key", "deps", "signal", "val", "sem")

    def __init__(self, eng, fn, reads, writes, dma_key):
        self.eng = eng
        self.fn = fn
        self.reads = reads
        self.writes = writes
        self.dma_key = dma_key
        self.deps = []
        self.signal = dma_key is not None
        self.val = 0
        self.sem = None


class Prog:
    ENGS = ("pe", "act", "dve", "pool", "sp")

    def __init__(self, nc, es):
        self.nc = nc
        self.es = es
        self.ops = []
        self.eng_obj = {"pe": nc.tensor, "act": nc.scalar, "dve": nc.vector,
                        "pool": nc.gpsimd, "sp": nc.sync}
        self.nbuf = 0

    def buf(self, name=None):
        self.nbuf += 1
        return Buf(name or f"b{self.nbuf}")

    def op(self, eng, fn, reads=(), writes=()):
        o = _Op(eng, fn, tuple(reads), tuple(writes), None)
        self.ops.append(o)
        return o

    def dma(self, eng, fn, reads=(), writes=(), key=None):
        if key is None:
            key = writes[0] if writes else reads[0]
        o = _Op(eng, fn, tuple(reads), tuple(writes), key)
        self.ops.append(o)
        return o

    def flush(self, barrier=True):
        nc = self.nc
        ops = self.ops
        self.ops = []
        if not hasattr(self, "eng_sem"):
            self.eng_sem, self.eng_cnt = {}, {}
            self.dma_sems, self.dma_cnt = [], {}
            self.dma_free = []
            self.key2sem = {}
            self.known = {e: {} for e in self.ENGS}
            self.n_wait = 0
        bufs = {}
        last = {}
        for o in ops:
            deps = []
            for b in o.reads:
                bufs[id(b)] = b
                if b.w is not None:
                    deps.append(b.w)
            for b in o.writes:
                bufs[id(b)] = b
                if b.w is not None:
                    deps.append(b.w)
                deps.extend(b.r)
            for b in o.reads:
                b.r.append(o)
            for b in o.writes:
                b.w = o
                b.r = []
            seen = set()
            for d in deps:
                if d is o or id(d) in seen:
                    continue
                seen.add(id(d))
                if d.dma_key is None and d.eng == "pe" and o.eng == "pe" and o.dma_key is None:
                    continue
                o.deps.append(d)
                d.signal = True
            if o.dma_key is None:
                last[o.eng] = o
        if barrier:
            for o in last.values():
                o.signal = True
        for o in ops:
            if not o.signal:
                continue
            if o.dma_key is not None:
                k = id(o.dma_key)
                if k not in self.key2sem:
                    if self.dma_free:
                        s = self.dma_free.pop()
                    else:
                        s = self.es.enter_context(nc.semaphore(f"dq{len(self.dma_sems)}"))
                        self.dma_sems.append(s)
                        self.dma_cnt[id(s)] = 0
                    self.key2sem[k] = s
                s = self.key2sem[k]
                self.dma_cnt[id(s)] += 16
                o.sem = s
                o.val = self.dma_cnt[id(s)]
            else:
                if o.eng not in self.eng_sem:
                    self.eng_sem[o.eng] = self.es.enter_context(nc.semaphore(f"e_{o.eng}"))
                    self.eng_cnt[o.eng] = 0
                self.eng_cnt[o.eng] += 1
                o.sem = self.eng_sem[o.eng]
                o.val = self.eng_cnt[o.eng]
        for o in ops:
            e = self.eng_obj[o.eng]
            kn = self.known[o.eng]
            need = {}
            for d in o.deps:
                sid = id(d.sem)
                if kn.get(sid, 0) >= d.val:
                    continue
                if sid not in need or need[sid][1] < d.val:
                    need[sid] = (d.sem, d.val)
            for sid, (sem, val) in need.items():
                e.wait_ge(sem, val)
                kn[sid] = val
                self.n_wait += 1
            inst = o.fn(e)
            if o.signal:
                inst.then_inc(o.sem, 16 if o.dma_key is not None else 1)
        if barrier:
            for en in self.ENGS:
                e = self.eng_obj[en]
                kn = self.known[en]
                for en2, sem in self.eng_sem.items():
                    if en2 == en:
                        continue
                    v = self.eng_cnt[en2]
                    if kn.get(id(sem), 0) < v:
                        e.wait_ge(sem, v)
                        kn[id(sem)] = v
                for sem in self.dma_sems:
                    v = self.dma_cnt[id(sem)]
                    if v > 0 and kn.get(id(sem), 0) < v:
                        e.wait_ge(sem, v)
                        kn[id(sem)] = v
            for b in bufs.values():
                b.w = None
                b.r = []
            self.dma_free = list(self.dma_sems)
            self.key2sem = {}
        return self

    def finalize(self):
        return self.flush(barrier=True)


class Ctx:
    def __init__(self):
        self.nc = bass.Bass("TRN2", target_bir_lowering=False)
        self.es = ExitStack()
        self.P = Prog(self.nc, self.es)
        self._n = 0
        self.scopes = []

    def sb(self, shape, dt, name=None):
        self._n += 1
        es = self.scopes[-1] if self.scopes else self.es
        t = es.enter_context(self.nc.sbuf_tensor(f"{name or 'sb'}_{self._n}", list(shape), dt))
        return t

    def push_scope(self):
        self.scopes.append(ExitStack())

    def pop_scope(self):
        self.P.flush(barrier=True)
        self.scopes.pop().close()

    def ps(self, shape, dt=F32, name=None):
        self._n += 1
        t = self.es.enter_context(self.nc.psum_tensor(name or f"ps{self._n}", list(shape), dt))
        return t

    def dram_in(self, name, shape, dt=F32):
        return self.nc.dram_tensor(name, list(shape), dt, kind="ExternalInput").ap()

    def dram_out(self, name, shape, dt=F32):
        return self.nc.dram_tensor(name, list(shape), dt, kind="ExternalOutput").ap()

    def dram_tmp(self, name, shape, dt=F32):
        return self.nc.dram_tensor(name, list(shape), dt, kind="Internal").ap()

    def finish(self):
        self.P.finalize()
        self.es.close()
        return self.nc


class Slots:
    def __init__(self, cx, n, shape, dt, name):
        self.t = [cx.sb(shape, dt, f"{name}{i}") for i in range(n)]
        self.b = [cx.P.buf(f"{name}{i}") for i in range(n)]
        self.i = 0
        self.n = n

    def next(self):
        i = self.i
        self.i = (self.i + 1) % self.n
        return self.t[i], self.b[i]


class Common:
    def __init__(self, cx):
        P = cx.P
        self.cx = cx
        self.banks = [cx.ps([128, 512], F32, f"bank{i}") for i in range(8)]
        self.bankb = [P.buf(f"bank{i}") for i in range(8)]
        self.ones_f = cx.sb([128, 128], F32, "ones_f")
        self.ones_fb = P.buf("ones_f")
        P.op("pool", lambda e: e.memset(self.ones_f[:], 1.0), writes=[self.ones_fb])
        self.ones_h = cx.sb([128, 128], BF16, "ones_h")
        self.ones_hb = P.buf("ones_h")
        P.op("pool", lambda e: e.memset(self.ones_h[:], 1.0), writes=[self.ones_hb])
        self.eps = cx.sb([128, 1], F32, "eps_c")
        self.epsb = P.buf("eps")
        P.op("pool", lambda e: e.memset(self.eps[:], EPS), writes=[self.epsb])


NC_D = D_MODEL // 128
NC_F = D_FF // 128


class FFNRes:
    def __init__(self, cx, cm, TB):
        self.TB = TB
        self.xn = cx.sb([128, NC_D, TB], BF16, "ffn_xn")
        self.xnb = [cx.P.buf(f"xn{c}") for c in range(NC_D)]
        self.xst = Slots(cx, 3, [128, TB], F32, "ffn_xst")
        self.sq = Slots(cx, 2, [128, TB], F32, "ffn_sq")
        self.rstd = cx.sb([128, TB], F32, "ffn_rstd")
        self.rstdb = cx.P.buf("rstd")
        self.g = cx.sb([128, NC_D], F32, "ffn_g")
        self.gb = cx.P.buf("ffn_g")
        self.wgu = Slots(cx, 4, [128, NC_D * 128], BF16, "ffn_wgu")
        self.sl = Slots(cx, 2, [128, TB], F32, "ffn_silu")
        self.ost = Slots(cx, 2, [128, TB], F32, "ffn_ost")


def rms_stats(cx, cm, T0, TB, xT_in, xst, sq, rstd, rstdb, banks, n_feat_chunks=NC_D, dim=D_MODEL):
    P = cx.P
    nb = TB // 512
    for c in range(n_feat_chunks):
        xt, xb = xst.next()
        P.dma("sp", lambda e, xt=xt, c=c: e.dma_start(out=xt[:], in_=xT_in[c * 128:(c + 1) * 128, T0:T0 + TB]),
              writes=[xb])
        st, sb_ = sq.next()
        P.op("act", lambda e, st=st, xt=xt: e.activation(out=st[:], in_=xt[:], func=AF.Square),
             reads=[xb], writes=[sb_])
        for h in range(nb):
            bk = banks[h]
            P.op("pe", lambda e, bk=bk, st=st, h=h, c=c: e.matmul(
                cm.banks[bk][:], cm.ones_f[:], st[:, h * 512:(h + 1) * 512],
                start=(c == 0), stop=(c == n_feat_chunks - 1)),
                reads=[sb_, cm.ones_fb], writes=[cm.bankb[bk]])
    for h in range(nb):
        bk = banks[h]
        P.op("act", lambda e, bk=bk, h=h: e.activation(
            out=rstd[:, h * 512:(h + 1) * 512], in_=cm.banks[bk][:], func=AF.Ln,
            bias=cm.eps[:], scale=1.0 / dim),
            reads=[cm.bankb[bk], cm.epsb], writes=[rstdb])
    P.op("act", lambda e: e.activation(out=rstd[:], in_=rstd[:], func=AF.Exp, scale=-0.5),
         reads=[rstdb], writes=[rstdb])


def ffn_phase(cx, cm, R, xT_in, xT_out, g_t, wg_t, wu_t, wd_t, T, final_g_t=None):
    P = cx.P
    TB = R.TB
    nb = TB // 512
    cx.push_scope()
    R.hT = cx.sb([128, NC_F, TB], BF16, "ffn_hT")
    R.hTb = [cx.P.buf(f"hT{f}") for f in range(NC_F)]
    R.wd = Slots(cx, 2, [128, NC_F * 128], BF16, "ffn_wd")
    P.dma("sp", lambda e: e.dma_start(out=R.g[:], in_=g_t[:, :]), writes=[R.gb])
    def _blk(T0):
        rms_stats(cx, cm, T0, TB, xT_in, R.xst, R.sq, R.rstd, R.rstdb, banks=[0, 1])
        for c in range(NC_D):
            xt, xb = R.xst.next()
            P.dma("sp", lambda e, xt=xt, c=c: e.dma_start(out=xt[:], in_=xT_in[c * 128:(c + 1) * 128, T0:T0 + TB]),
                  writes=[xb])
            P.op("dve", lambda e, xt=xt, c=c: e.scalar_tensor_tensor(
                out=R.xn[:, c, :], in0=xt[:], scalar=R.g[:, c:c + 1], in1=R.rstd[:],
                op0=ALU.mult, op1=ALU.mult),
                reads=[xb, R.gb, R.rstdb], writes=[R.xnb[c]])
        pend = None
        for f in range(NC_F):
            wgt, wgb = R.wgu.next()
            wut, wub = R.wgu.next()
            P.dma("pool", lambda e, wgt=wgt, f=f: e.dma_start(out=wgt[:], in_=wg_t[f]), writes=[wgb])
            P.dma("pool", lambda e, wut=wut, f=f: e.dma_start(out=wut[:], in_=wu_t[f]), writes=[wub])
            par = f % 2
            gbk = [0 + par * 4 + h for h in range(nb)]
            ubk = [2 + par * 4 + h for h in range(nb)]
            for (wt, wb, bks) in ((wgt, wgb, gbk), (wut, wub, ubk)):
                for c in range(NC_D):
                    for h in range(nb):
                        P.op("pe", lambda e, wt=wt, c=c, h=h, bks=bks: e.matmul(
                            cm.banks[bks[h]][:], wt[:, c * 128:(c + 1) * 128],
                            R.xn[:, c, h * 512:(h + 1) * 512],
                            start=(c == 0), stop=(c == NC_D - 1)),
                            reads=[wb, R.xnb[c]], writes=[cm.bankb[bks[h]]])
            slt, slb = R.sl.next()
            for h in range(nb):
                P.op("act", lambda e, slt=slt, h=h, gbk=gbk: e.activation(
                    out=slt[:, h * 512:(h + 1) * 512], in_=cm.banks[gbk[h]][:], func=AF.Silu),
                    reads=[cm.bankb[gbk[h]]], writes=[slb])
            for h in range(nb):
                P.op("dve", lambda e, slt=slt, h=h, ubk=ubk, f=f: e.tensor_tensor(
                    out=R.hT[:, f, h * 512:(h + 1) * 512], in0=slt[:, h * 512:(h + 1) * 512],
                    in1=cm.banks[ubk[h]][:], op=ALU.mult),
                    reads=[slb, cm.bankb[ubk[h]]], writes=[R.hTb[f]])
        for dt in range(NC_D):
            wt, wb = R.wd.next()
            P.dma("pool", lambda e, wt=wt, dt=dt: e.dma_start(out=wt[:], in_=wd_t[dt]), writes=[wb])
            par = dt % 2
            ybk = [par * 2 + h for h in range(nb)]
            for f in range(NC_F):
                for h in range(nb):
                    P.op("pe", lambda e, wt=wt, f=f, h=h, ybk=ybk: e.matmul(
                        cm.banks[ybk[h]][:], wt[:, f * 128:(f + 1) * 128],
                        R.hT[:, f, h * 512:(h + 1) * 512],
                        start=(f == 0), stop=(f == NC_F - 1)),
                        reads=[wb, R.hTb[f]], writes=[cm.bankb[ybk[h]]])
            xt, xb = R.xst.next()
            P.dma("sp", lambda e, xt=xt, dt=dt: e.dma_start(out=xt[:], in_=xT_in[dt * 128:(dt + 1) * 128, T0:T0 + TB]),
                  writes=[xb])
            ot, ob = R.ost.next()
            for h in range(nb):
                P.op("dve", lambda e, ot=ot, xt=xt, h=h, ybk=ybk: e.scalar_tensor_tensor(
                    out=ot[:, h * 512:(h + 1) * 512], in0=cm.banks[ybk[h]][:], scalar=0.5,
                    in1=xt[:, h * 512:(h + 1) * 512], op0=ALU.mult, op1=ALU.add),
                    reads=[cm.bankb[ybk[h]], xb], writes=[ob])
            P.dma("sp", lambda e, ot=ot, dt=dt: e.dma_start(out=xT_out[dt * 128:(dt + 1) * 128, T0:T0 + TB], in_=ot[:]),
                  reads=[ob], key=ob)
    for T0_ in range(0, T, TB):
        _blk(T0_)
    cx.pop_scope()


def tile_w_in(w, nk, nf):
    K, Fd = w.shape
    return np.ascontiguousarray(w.reshape(nk, 128, nf, 128).transpose(2, 1, 0, 3).reshape(nf, 128, nk * 128))


def vec_t(g, nchunk):
    return np.ascontiguousarray(g.reshape(nchunk, 128).T)


def norm_to_xn(cx, cm, R, xT_in, g_tile, gb, T0, TB):
    P = cx.P
    rms_stats(cx, cm, T0, TB, xT_in, R.xst, R.sq, R.rstd, R.rstdb, banks=[0, 1])
    for c in range(NC_D):
        xt, xb = R.xst.next()
        P.dma("sp", lambda e, xt=xt, c=c: e.dma_start(out=xt[:, :TB], in_=xT_in[c * 128:(c + 1) * 128, T0:T0 + TB]),
              writes=[xb])
        P.op("dve", lambda e, xt=xt, c=c: e.scalar_tensor_tensor(
            out=R.xn[:, c, :TB], in0=xt[:, :TB], scalar=g_tile[:, c:c + 1], in1=R.rstd[:, :TB],
            op0=ALU.mult, op1=ALU.mult),
            reads=[xb, gb, R.rstdb], writes=[R.xnb[c]])


def proj_tile(cx, cm, R, w_t, f, bks, TB):
    P = cx.P
    nb = TB // 512
    wt, wb = R.wgu.next()
    P.dma("pool", lambda e, wt=wt, f=f: e.dma_start(out=wt[:], in_=w_t[f]), writes=[wb])
    for c in range(NC_D):
        for h in range(nb):
            P.op("pe", lambda e, wt=wt, c=c, h=h, bks=bks: e.matmul(
                cm.banks[bks[h]][:], wt[:, c * 128:(c + 1) * 128],
                R.xn[:, c, h * 512:(h + 1) * 512],
                start=(c == 0), stop=(c == NC_D - 1)),
                reads=[wb, R.xnb[c]], writes=[cm.bankb[bks[h]]])


def store_rows(cx, ot, ob, dst, r0, T0, TB, nrows=128):
    cx.P.dma("sp", lambda e: e.dma_start(out=dst[r0:r0 + nrows, T0:T0 + TB], in_=ot[:nrows, :TB]),
             reads=[ob], key=ob)


def outproj_phase(cx, cm, R, catT, w_t, xT_in, xT_out, T):
    P = cx.P
    TB = R.TB
    nb = TB // 512
    def _blk(T0):
        for c in range(NC_D):
            P.dma("pool", lambda e, c=c: e.dma_start(out=R.xn[:, c, :TB], in_=catT[c * 128:(c + 1) * 128, T0:T0 + TB]),
                  writes=[R.xnb[c]])
        for dt in range(NC_D):
            bks = [(dt % 2) * 2 + h for h in range(nb)]
            proj_tile(cx, cm, R, w_t, dt, bks, TB)
            xt, xb = R.xst.next()
            P.dma("sp", lambda e, xt=xt, dt=dt: e.dma_start(out=xt[:, :TB], in_=xT_in[dt * 128:(dt + 1) * 128, T0:T0 + TB]),
                  writes=[xb])
            ot, ob = R.ost.next()
            for h in range(nb):
                P.op("dve", lambda e, ot=ot, xt=xt, h=h, bks=bks: e.tensor_tensor(
                    out=ot[:, h * 512:(h + 1) * 512], in0=cm.banks[bks[h]][:],
                    in1=xt[:, h * 512:(h + 1) * 512], op=ALU.add),
                    reads=[cm.bankb[bks[h]], xb], writes=[ob])
            store_rows(cx, ot, ob, xT_out, dt * 128, T0, TB)
    for T0_ in range(0, T, TB):
        _blk(T0_)
    cx.P.flush(barrier=True)


def final_norm_phase(cx, cm, R, xT_in, g_t, xT_out, T):
    P = cx.P
    TB = R.TB
    P.dma("sp", lambda e: e.dma_start(out=R.g[:], in_=g_t[:, :]), writes=[R.gb])
    def _blk(T0):
        rms_stats(cx, cm, T0, TB, xT_in, R.xst, R.sq, R.rstd, R.rstdb, banks=[0, 1])
        for c in range(NC_D):
            xt, xb = R.xst.next()
            P.dma("sp", lambda e, xt=xt, c=c: e.dma_start(out=xt[:, :TB], in_=xT_in[c * 128:(c + 1) * 128, T0:T0 + TB]),
                  writes=[xb])
            ot, ob = R.ost.next()
            P.op("dve", lambda e, xt=xt, ot=ot, c=c: e.scalar_tensor_tensor(
                out=ot[:, :TB], in0=xt[:, :TB], scalar=R.g[:, c:c + 1], in1=R.rstd[:, :TB],
                op0=ALU.mult, op1=ALU.mult),
                reads=[xb, R.gb, R.rstdb], writes=[ob])
            store_rows(cx, ot, ob, xT_out, c * 128, T0, TB)
    for T0_ in range(0, T, TB):
        _blk(T0_)


NEG = -60000.0
C_ID, C_E0, C_NE0, C_NS0, C_TI0, C_E1, C_NE1, C_NS1, C_TI1, C_ROT, C_INVF = range(11)
N_CONST = 11
ROPE_THETA = 500000.0


def make_consts():
    i = np.arange(128)
    ident = np.eye(128, dtype=np.float32)
    out = [ident]
    for d in range(2):
        if d == 0:
            be = i[:, None] <= i[None, :]
        else:
            be = i[:, None] >= i[None, :]
        E = be.astype(np.float32)
        ns = np.where(be.T & (i[:, None] != i[None, :]), 0.0, NEG).astype(np.float32)
        ti = np.where(be, 0.0, NEG).astype(np.float32)
        out += [E, -E, ns, ti]
    rot = np.zeros((128, 128), np.float32)
    for m in range(16):
        rot[m + 16, m] = -1.0
        rot[m, m + 16] = 1.0
    invf = np.zeros((128, 128), np.float32)
    fr = ROPE_THETA ** (-np.arange(0, 32, 2, dtype=np.float32) / np.float32(32))
    invf[0:16, 0] = fr
    invf[16:32, 0] = fr
    out += [rot, invf]
    return np.ascontiguousarray(np.concatenate(out, axis=1))


class Consts:
    def __init__(self, cx, consts_dram):
        self.t = cx.sb([128, N_CONST * 128], F32, "consts")
        self.b = cx.P.buf("consts")
        cx.P.dma("sp", lambda e: e.dma_start(out=self.t[:], in_=consts_dram[:, :]), writes=[self.b])

    def m(self, k):
        return self.t[:, k * 128:(k + 1) * 128]


def even_proj_phase(cx, cm, R, xT_in, g_t, w_main_t, w_ba_t, dtb_rep, alog_rep, aT, qkvT, zsT, bg_tok, T):
    P = cx.P
    TB = R.TB
    nb = TB // 512
    cx.push_scope()
    P.dma("sp", lambda e: e.dma_start(out=R.g[:], in_=g_t[:, :]), writes=[R.gb])
    wba = cx.sb([128, 16 * 32], BF16, "wba")
    wbab = P.buf("wba")
    P.dma("pool", lambda e: e.dma_start(out=wba[:], in_=w_ba_t[:, :]), writes=[wbab])
    dtb = cx.sb([128, 16], F32, "dtb")
    nega = cx.sb([128, 16], F32, "nega")
    dtbb, negab = P.buf("dtb"), P.buf("nega")
    P.dma("sp", lambda e: e.dma_start(out=dtb[:], in_=dtb_rep[:, :]), writes=[dtbb])
    P.dma("sp", lambda e: e.dma_start(out=nega[:], in_=alog_rep[:, :]), writes=[negab])
    P.op("act", lambda e: e.activation(out=nega[:], in_=nega[:], func=AF.Exp), reads=[negab], writes=[negab])
    P.op("dve", lambda e: e.tensor_scalar(out=nega[:], in0=nega[:], scalar1=-1.0, scalar2=None, op0=ALU.mult),
         reads=[negab], writes=[negab])
    bgs = Slots(cx, 2, [128, 32], F32, "bgst")
    t1s = Slots(cx, 2, [128, 16], F32, "bgt1")
    def _blk(T0):
        norm_to_xn(cx, cm, R, xT_in, R.g, R.gb, T0, TB)
        for tt in range(TB // 128):
            bk = 4 + (tt % 2)
            for c in range(NC_D):
                P.op("pe", lambda e, c=c, tt=tt, bk=bk: e.matmul(
                    cm.banks[bk][:, 0:32], R.xn[:, c, tt * 128:(tt + 1) * 128], wba[:, c * 32:(c + 1) * 32],
                    start=(c == 0), stop=(c == NC_D - 1)),
                    reads=[R.xnb[c], wbab], writes=[cm.bankb[bk]])
            bt, bb = bgs.next()
            t1, t1b = t1s.next()
            P.op("dve", lambda e, t1=t1, bk=bk: e.tensor_tensor(out=t1[:], in0=cm.banks[bk][:, 16:32], in1=dtb[:], op=ALU.add),
                 reads=[cm.bankb[bk], dtbb], writes=[t1b])
            P.op("act", lambda e, t1=t1: e.activation(out=t1[:], in_=t1[:], func=AF.Exp), reads=[t1b], writes=[t1b])
            P.op("act", lambda e, t1=t1: e.activation(out=t1[:], in_=t1[:], func=AF.Ln, bias=1.0), reads=[t1b], writes=[t1b])
            P.op("dve", lambda e, t1=t1, bt=bt: e.tensor_tensor(out=bt[:, 16:32], in0=t1[:], in1=nega[:], op=ALU.mult),
                 reads=[t1b, negab], writes=[bb])
            P.op("act", lambda e, bt=bt, bk=bk: e.activation(out=bt[:, 0:16], in_=cm.banks[bk][:, 0:16], func=AF.Sigmoid),
                 reads=[cm.bankb[bk]], writes=[bb])
            P.dma("sp", lambda e, bt=bt, tt=tt: e.dma_start(out=bg_tok[T0 + tt * 128:T0 + (tt + 1) * 128, :], in_=bt[:]),
                  reads=[bb], key=bb)
        for j in range(8):
            vb_ = [0, 1][:nb]
            gb_ = [2, 3][:nb]
            proj_tile(cx, cm, R, w_main_t, j, vb_, TB)
            proj_tile(cx, cm, R, w_main_t, 8 + j, gb_, TB)
            slt, slb = R.sl.next()
            ot, ob = R.ost.next()
            for h in range(nb):
                P.op("act", lambda e, slt=slt, h=h, gb_=gb_: e.activation(
                    out=slt[:, h * 512:(h + 1) * 512], in_=cm.banks[gb_[h]][:], func=AF.Sigmoid),
                    reads=[cm.bankb[gb_[h]]], writes=[slb])
                P.op("dve", lambda e, slt=slt, ot=ot, h=h, vb_=vb_: e.tensor_tensor(
                    out=ot[:, h * 512:(h + 1) * 512], in0=slt[:, h * 512:(h + 1) * 512],
                    in1=cm.banks[vb_[h]][:], op=ALU.mult),
                    reads=[slb, cm.bankb[vb_[h]]], writes=[ob])
            store_rows(cx, ot, ob, aT, j * 128, T0, TB)
        for f in range(16, 48):
            bks = [4 + (f % 2) * 2 + h for h in range(nb)]
            proj_tile(cx, cm, R, w_main_t, f, bks, TB)
            ot, ob = R.ost.next()
            fn = AF.Copy if f < 40 else AF.Silu
            for h in range(nb):
                P.op("act", lambda e, ot=ot, h=h, bks=bks, fn=fn: e.activation(
                    out=ot[:, h * 512:(h + 1) * 512], in_=cm.banks[bks[h]][:], func=fn),
                    reads=[cm.bankb[bks[h]]], writes=[ob])
            if f < 40:
                store_rows(cx, ot, ob, qkvT, (f - 16) * 128, T0, TB)
            else:
                store_rows(cx, ot, ob, zsT, (f - 40) * 128, T0, TB)
    for T0_ in range(0, T, TB):
        _blk(T0_)
    cx.pop_scope()


def conv_module_phase(cx, cm, K, a_halo, cw_t, cb_t, lng_t, lnb_t, a_fin, T):
    P = cx.P
    NCG, KW = 8, 31
    nblk = T // 512
    cw = cx.sb([128, NCG * KW], F32, "cw")
    prm = cx.sb([128, 24], F32, "cprm")
    cwb, prmb = P.buf("cw"), P.buf("cprm")
    P.dma("sp", lambda e: e.dma_start(out=cw[:], in_=cw_t[:, :]), writes=[cwb])
    P.dma("sp", lambda e: e.dma_start(out=prm[:, 0:8], in_=cb_t[:, :]), writes=[prmb])
    P.dma("sp", lambda e: e.dma_start(out=prm[:, 8:16], in_=lng_t[:, :]), writes=[prmb])
    P.dma("sp", lambda e: e.dma_start(out=prm[:, 16:24], in_=lnb_t[:, :]), writes=[prmb])
    ycv = cx.sb([128, NCG, T], F32, "ycv")
    ycvb = [P.buf(f"ycv{g}") for g in range(NCG)]
    ah = Slots(cx, 2, [128, T + 30], BF16, "ah")
    dg = cx.sb([128, KW * 128], BF16, "dg")
    dgb = P.buf("dg")
    for cg in range(NCG):
        at, ab = ah.next()
        P.dma("pool", lambda e, at=at, cg=cg: e.dma_start(out=at[:], in_=a_halo[cg * 128:(cg + 1) * 128, :]), writes=[ab])
        for k in range(KW):
            P.op("dve", lambda e, k=k, cg=cg: e.tensor_scalar(
                out=dg[:, k * 128:(k + 1) * 128], in0=K.m(C_ID), scalar1=cw[:, cg * KW + k:cg * KW + k + 1],
                scalar2=None, op0=ALU.mult), reads=[K.b, cwb], writes=[dgb])
        for tb in range(nblk):
            bk = tb % 4
            for k in range(KW):
                P.op("pe", lambda e, k=k, tb=tb, bk=bk, at=at: e.matmul(
                    cm.banks[bk][:], dg[:, k * 128:(k + 1) * 128], at[:, tb * 512 + k:tb * 512 + k + 512],
                    start=(k == 0), stop=(k == KW - 1)), reads=[dgb, ab], writes=[cm.bankb[bk]])
            P.op("act", lambda e, cg=cg, tb=tb, bk=bk: e.activation(
                out=ycv[:, cg, tb * 512:(tb + 1) * 512], in_=cm.banks[bk][:], func=AF.Identity,
                bias=prm[:, cg:cg + 1]), reads=[cm.bankb[bk], prmb], writes=[ycvb[cg]])
    mean = cx.sb([128, T], F32, "cmean")
    meanb = P.buf("cmean")
    rstd = cx.sb([128, T], F32, "crstd")
    rstdb = P.buf("crstd")
    for tb in range(nblk):
        bk = 4 + tb % 4
        for cg in range(NCG):
            P.op("pe", lambda e, cg=cg, tb=tb, bk=bk: e.matmul(
                cm.banks[bk][:], cm.ones_f[:], ycv[:, cg, tb * 512:(tb + 1) * 512],
                start=(cg == 0), stop=(cg == NCG - 1)), reads=[ycvb[cg], cm.ones_fb], writes=[cm.bankb[bk]])
        P.op("act", lambda e, tb=tb, bk=bk: e.activation(
            out=mean[:, tb * 512:(tb + 1) * 512], in_=cm.banks[bk][:], func=AF.Copy, scale=1.0 / 1024),
            reads=[cm.bankb[bk]], writes=[meanb])
    for cg in range(NCG):
        P.op("dve", lambda e, cg=cg: e.tensor_tensor(out=ycv[:, cg, :], in0=ycv[:, cg, :], in1=mean[:], op=ALU.subtract),
             reads=[ycvb[cg], meanb], writes=[ycvb[cg]])
    sq = Slots(cx, 2, [128, 512], F32, "csq")
    for tb in range(nblk):
        bk = tb % 4
        for cg in range(NCG):
            st, sb_ = sq.next()
            P.op("act", lambda e, st=st, cg=cg, tb=tb: e.activation(
                out=st[:], in_=ycv[:, cg, tb * 512:(tb + 1) * 512], func=AF.Square), reads=[ycvb[cg]], writes=[sb_])
            P.op("pe", lambda e, st=st, cg=cg, bk=bk: e.matmul(
                cm.banks[bk][:], cm.ones_f[:], st[:], start=(cg == 0), stop=(cg == NCG - 1)),
                reads=[sb_, cm.ones_fb], writes=[cm.bankb[bk]])
        P.op("act", lambda e, tb=tb, bk=bk: e.activation(
            out=rstd[:, tb * 512:(tb + 1) * 512], in_=cm.banks[bk][:], func=AF.Ln, bias=cm.eps[:], scale=1.0 / 1024),
            reads=[cm.bankb[bk], cm.epsb], writes=[rstdb])
    P.op("act", lambda e: e.activation(out=rstd[:], in_=rstd[:], func=AF.Exp, scale=-0.5), reads=[rstdb], writes=[rstdb])
    tmp = Slots(cx, 2, [128, T], F32, "ctmp")
    ost = Slots(cx, 2, [128, T], F32, "cost")
    for cg in range(NCG):
        tt, tb_ = tmp.next()
        P.op("dve", lambda e, tt=tt, cg=cg: e.tensor_tensor(out=tt[:], in0=ycv[:, cg, :], in1=rstd[:], op=ALU.mult),
             reads=[ycvb[cg], rstdb], writes=[tb_])
        ot, ob = ost.next()
        P.op("act", lambda e, tt=tt, ot=ot, cg=cg: e.activation(
            out=ot[:], in_=tt[:], func=AF.Silu, scale=prm[:, 8 + cg:9 + cg], bias=prm[:, 16 + cg:17 + cg]),
            reads=[tb_, prmb], writes=[ob])
        P.dma("sp", lambda e, ot=ot, cg=cg: e.dma_start(out=a_fin[cg * 128:(cg + 1) * 128, :], in_=ot[:]),
              reads=[ob], key=ob)


def deltanet_phase(cx, cm, K, qkv_h, zs_h, bg_h, sw_t, og_t, o_h, S, nh):
    P = cx.P
    NCH = S // 128
    NG = NCH // 4
    raw = cx.sb([128, S + 2], F32, "dn_raw"); rawb = P.buf("raw")
    tmpT = cx.sb([128, S], F32, "dn_tmpT"); tmpTb = P.buf("tmpT")
    qT = cx.sb([128, S], F32, "dn_qT"); qTb = P.buf("qT")
    kT = cx.sb([128, S], F32, "dn_kT"); kTb = P.buf("kT")
    ktok = cx.sb([128, NCH, 128], F32, "dn_ktok"); ktokb = [P.buf(f"ktok{g}") for g in range(NG)]
    vtok = cx.sb([128, NCH, 128], F32, "dn_vtok"); vtokb = [P.buf(f"vtok{g}") for g in range(NG)]
    AT = cx.sb([128, NCH, 128], F32, "dn_AT"); ATb = [P.buf(f"AT{g}") for g in range(NG)]
    u = raw[:, 0:S].rearrange("p (n c) -> p n c", c=128); ub = [P.buf(f"u{g}") for g in range(NG)]
    wT = tmpT[:, :].rearrange("p (n c) -> p n c", c=128); wTb = [P.buf(f"wT{g}") for g in range(NG)]
    rawfree, tmpTfree = P.buf("rawfree"), P.buf("tmpTfree")
    oacc = cx.sb([128, NCH, 128], F32, "dn_oacc"); oaccb = [P.buf(f"oacc{n}") for n in range(NCH)]
    bgs = cx.sb([128, NCH, 4 * nh], F32, "dn_bgs"); bgsb = P.buf("bgs")
    sw = cx.sb([128, nh * 9], F32, "dn_sw"); swb = P.buf("sw")
    og = cx.sb([128, 1], F32, "dn_og"); ogb = P.buf("og")
    I4 = cx.sb([128, 512], F32, "dn_I4"); I4b = P.buf("I4")
    P.dma("sp", lambda e: e.dma_start(out=bgs[:], in_=bg_h.rearrange("(n p) c -> p n c", p=128)), writes=[bgsb])
    P.dma("sp", lambda e: e.dma_start(out=sw[:], in_=sw_t[:, :]), writes=[swb])
    P.dma("sp", lambda e: e.dma_start(out=og[:], in_=og_t[:, :]), writes=[ogb])
    for c in range(4):
        P.op("dve", lambda e, c=c: e.tensor_copy(I4[:, c * 128:(c + 1) * 128], K.m(C_ID)), reads=[K.b], writes=[I4b])
    sqs = Slots(cx, 2, [128, 512], F32, "dn_sq")
    rns = Slots(cx, 2, [128, 512], F32, "dn_rn")
    col = {k: (cx.sb([128, NCH], F32, f"dn_{k}"), P.buf(k)) for k in
           ("bcol", "gcol", "gc", "gexp", "glexp", "kdf", "bg2", "ss")}
    G1 = cx.sb([128, 4, 128], F32, "dn_G1"); G1b = P.buf("G1")
    Dn = cx.sb([128, 512], F32, "dn_Dn"); Dnb = P.buf("Dn")
    DTi = cx.sb([128, 512], F32, "dn_DTi"); DTib = P.buf("DTi")
    Lx = Slots(cx, 2, [128, 512], F32, "dn_L")
    Mx = Slots(cx, 2, [128, 512], F32, "dn_M")
    Px = Slots(cx, 2, [128, 512], F32, "dn_P")
    vbs = Slots(cx, 2, [128, 512], F32, "dn_vb")
    kbgs = Slots(cx, 2, [128, 512], F32, "dn_kbg")
    Sst = Slots(cx, 2, [128, 128], F32, "dn_S")
    vns = Slots(cx, 2, [128, 128], F32, "dn_vn")
    kds = Slots(cx, 2, [128, 128], F32, "dn_kd")
    t1s = Slots(cx, 2, [128, 128], F32, "dn_t1")
    t2s = Slots(cx, 2, [128, 128], F32, "dn_t2")
    on4s = Slots(cx, 2, [128, 512], F32, "dn_on4")
    osts = Slots(cx, 2, [128, 512], F32, "dn_ost")
    ident = K.m(C_ID)
    bk, bkb = cm.banks, cm.bankb

    def cs(c):
        return slice(c * 128, (c + 1) * 128)

    for hh in range(nh):
        for part, (dst, dstb) in enumerate(((qT, qTb), (kT, kTb), (tmpT, tmpTb))):
            P.dma("sp", lambda e, part=part, hh=hh: e.dma_start(out=raw[:], in_=qkv_h[part, hh]), writes=[rawb] + ub)
            wc = hh * 9 + part * 3
            xw = wTb if part == 2 else []
            P.op("dve", lambda e, dst=dst, wc=wc: e.tensor_scalar(
                out=dst[:], in0=raw[:, 0:S], scalar1=sw[:, wc:wc + 1], scalar2=None, op0=ALU.mult),
                reads=[rawb, swb], writes=[dstb] + xw)
            for tap in (1, 2):
                P.op("dve", lambda e, dst=dst, wc=wc, tap=tap: e.scalar_tensor_tensor(
                    out=dst[:], in0=raw[:, tap:tap + S], scalar=sw[:, wc + tap:wc + tap + 1], in1=dst[:],
                    op0=ALU.mult, op1=ALU.add), reads=[rawb, swb, dstb], writes=[dstb, rawfree])
            P.op("act", lambda e, dst=dst: e.activation(out=dst[:], in_=dst[:], func=AF.Silu), reads=[dstb], writes=[dstb])
            if part < 2:
                for blk in range(S // 512):
                    bs = slice(blk * 512, (blk + 1) * 512)
                    st, sb_ = sqs.next()
                    rn, rnb = rns.next()
                    b_ = blk % 2
                    P.op("act", lambda e, st=st, dst=dst, bs=bs: e.activation(out=st[:], in_=dst[:, bs], func=AF.Square),
                         reads=[dstb], writes=[sb_])
                    P.op("pe", lambda e, st=st, b_=b_: e.matmul(bk[b_][:], cm.ones_f[:], st[:], start=True, stop=True),
                         reads=[sb_, cm.ones_fb], writes=[bkb[b_]])
                    P.op("act", lambda e, rn=rn, b_=b_: e.activation(out=rn[:], in_=bk[b_][:], func=AF.Ln, bias=cm.eps[:]),
                         reads=[bkb[b_], cm.epsb], writes=[rnb])
                    P.op("act", lambda e, rn=rn: e.activation(out=rn[:], in_=rn[:], func=AF.Exp, scale=-0.5),
                         reads=[rnb], writes=[rnb])
                    sc = (128.0 ** -0.5) if part == 0 else 1.0
                    P.op("dve", lambda e, dst=dst, bs=bs, rn=rn, sc=sc: e.scalar_tensor_tensor(
                        out=dst[:, bs], in0=dst[:, bs], scalar=sc, in1=rn[:], op0=ALU.mult, op1=ALU.mult),
                        reads=[dstb, rnb], writes=[dstb])
        for g in range(NG):
            for (src, srcb, dst, dstb, b_) in ((kT, kTb, ktok, ktokb, 2), (tmpT, tmpTb, vtok, vtokb, 3)):
                for c in range(4):
                    n = 4 * g + c
                    P.op("pe", lambda e, src=src, b_=b_, c=c, n=n: e.transpose(bk[b_][:, cs(c)], src[:, cs(n)], ident),
                         reads=[srcb, K.b], writes=[bkb[b_], tmpTfree])
                eng = "act" if b_ == 2 else "dve"
                if eng == "act":
                    P.op("act", lambda e, dst=dst, g=g, b_=b_: e.activation(
                        out=dst[:, 4 * g:4 * g + 4, :].rearrange("p a b -> p (a b)"), in_=bk[b_][:], func=AF.Copy),
                        reads=[bkb[b_]], writes=[dstb[g]])
                else:
                    P.op("dve", lambda e, dst=dst, g=g, b_=b_: e.tensor_copy(
                        dst[:, 4 * g:4 * g + 4, :].rearrange("p a b -> p (a b)"), bk[b_][:]),
                        reads=[bkb[b_]], writes=[dstb[g]])
        for d in range(2):
            E, NE, NS, TI = (K.m(C_E0 + 4 * d), K.m(C_NE0 + 4 * d), K.m(C_NS0 + 4 * d), K.m(C_TI0 + 4 * d))
            (bcol, bcolb), (gcol, gcolb), (gc, gcb), (gexp, gexpb) = col["bcol"], col["gcol"], col["gc"], col["gexp"]
            (glexp, glexpb), (kdf, kdfb), (bg2, bg2b) = col["glexp"], col["kdf"], col["bg2"]
            P.op("dve", lambda e, d=d, hh=hh: e.tensor_copy(bcol[:], bgs[:, :, d * nh + hh]), reads=[bgsb], writes=[bcolb])
            P.op("dve", lambda e, d=d, hh=hh: e.tensor_copy(gcol[:], bgs[:, :, 2 * nh + d * nh + hh]), reads=[bgsb], writes=[gcolb])
            P.op("pe", lambda e, E=E: e.matmul(bk[0][:, 0:NCH], E, gcol[:], start=True, stop=True),
                 reads=[K.b, gcolb], writes=[bkb[0]])
            P.op("pe", lambda e: e.matmul(bk[1][:, 0:NCH], cm.ones_f[:], gcol[:], start=True, stop=True),
                 reads=[cm.ones_fb, gcolb], writes=[bkb[1]])
            P.op("act", lambda e: e.activation(out=gc[:], in_=bk[0][:, 0:NCH], func=AF.Copy), reads=[bkb[0]], writes=[gcb])
            P.op("act", lambda e: e.activation(out=gexp[:], in_=bk[0][:, 0:NCH], func=AF.Exp), reads=[bkb[0]], writes=[gexpb])
            P.op("act", lambda e: e.activation(out=glexp[:], in_=bk[1][:, 0:NCH], func=AF.Exp), reads=[bkb[1]], writes=[glexpb])
            P.op("dve", lambda e: e.tensor_tensor(out=kdf[:], in0=bk[1][:, 0:NCH], in1=gc[:], op=ALU.subtract),
                 reads=[bkb[1], gcb], writes=[kdfb])
            P.op("act", lambda e: e.activation(out=kdf[:], in_=kdf[:], func=AF.Exp), reads=[kdfb], writes=[kdfb])
            P.op("dve", lambda e: e.tensor_tensor(out=bg2[:], in0=bcol[:], in1=gexp[:], op=ALU.mult),
                 reads=[bcolb, gexpb], writes=[bg2b])
            for g in range(NG):
                for c in range(4):
                    n = 4 * g + c
                    P.op("dve", lambda e, c=c, n=n: e.tensor_scalar(
                        out=G1[:, c, :], in0=cm.ones_f[:], scalar1=gcol[:, n:n + 1], scalar2=None, op0=ALU.mult),
                        reads=[cm.ones_fb, gcolb], writes=[G1b])
                for c in range(4):
                    P.op("pe", lambda e, c=c, E=E: e.matmul(bk[0][:, cs(c)], E, G1[:, c, :], start=True, stop=False),
                         reads=[K.b, G1b], writes=[bkb[0]])
                    P.op("pe", lambda e, c=c, NE=NE: e.matmul(bk[0][:, cs(c)], G1[:, c, :], NE, start=False, stop=False),
                         reads=[K.b, G1b], writes=[bkb[0]])
                    P.op("pe", lambda e, c=c, NS=NS: e.matmul(bk[0][:, cs(c)], ident, NS, start=False, stop=True),
                         reads=[K.b], writes=[bkb[0]])
                    P.op("pe", lambda e, c=c, E=E: e.matmul(bk[1][:, cs(c)], G1[:, c, :], E, start=True, stop=False),
                         reads=[K.b, G1b], writes=[bkb[1]])
                    P.op("pe", lambda e, c=c, NE=NE: e.matmul(bk[1][:, cs(c)], NE, G1[:, c, :], start=False, stop=False),
                         reads=[K.b, G1b], writes=[bkb[1]])
                    P.op("pe", lambda e, c=c, TI=TI: e.matmul(bk[1][:, cs(c)], ident, TI, start=False, stop=True),
                         reads=[K.b], writes=[bkb[1]])
                P.op("act", lambda e: e.activation(out=Dn[:], in_=bk[0][:], func=AF.Exp), reads=[bkb[0]], writes=[Dnb])
                P.op("act", lambda e: e.activation(out=DTi[:], in_=bk[1][:], func=AF.Exp), reads=[bkb[1]], writes=[DTib])
                for c in range(4):
                    n = 4 * g + c
                    P.op("pe", lambda e, c=c, n=n: e.matmul(bk[2][:, cs(c)], kT[:, cs(n)], kT[:, cs(n)], start=True, stop=True),
                         reads=[kTb], writes=[bkb[2]])
                    P.op("pe", lambda e, c=c, n=n: e.matmul(bk[3][:, cs(c)], kT[:, cs(n)], qT[:, cs(n)], start=True, stop=True),
                         reads=[kTb, qTb], writes=[bkb[3]])
                Lc, Lcb = Lx.next()
                Mc, Mcb = Mx.next()
                Pc, Pcb = Px.next()
                for c in range(4):
                    n = 4 * g + c
                    P.op("dve", lambda e, c=c, n=n, Lc=Lc: e.scalar_tensor_tensor(
                        out=Lc[:, cs(c)], in0=bk[2][:, cs(c)], scalar=bcol[:, n:n + 1], in1=Dn[:, cs(c)],
                        op0=ALU.mult, op1=ALU.mult), reads=[bkb[2], bcolb, Dnb], writes=[Lcb])
                P.op("dve", lambda e, g=g: e.tensor_tensor(
                    out=AT[:, 4 * g:4 * g + 4, :].rearrange("p a b -> p (a b)"), in0=bk[3][:], in1=DTi[:], op=ALU.mult),
                    reads=[bkb[3], DTib], writes=[ATb[g]])
                for c in range(4):
                    P.op("pe", lambda e, c=c, Lc=Lc: e.transpose(bk[4][:, cs(c)], Lc[:, cs(c)], ident),
                         reads=[Lcb, K.b], writes=[bkb[4]])
                P.op("act", lambda e, Mc=Mc: e.activation(out=Mc[:], in_=bk[4][:], func=AF.Copy), reads=[bkb[4]], writes=[Mcb])
                P.op("dve", lambda e, Mc=Mc, Pc=Pc: e.tensor_tensor(out=Pc[:], in0=I4[:], in1=Mc[:], op=ALU.subtract),
                     reads=[I4b, Mcb], writes=[Pcb])
                for lv in range(6):
                    lastlv = lv == 5
                    Ln_, Lnb_ = Lx.next()
                    for c in range(4):
                        P.op("pe", lambda e, c=c, Mc=Mc, Lc=Lc: e.matmul(bk[5][:, cs(c)], Mc[:, cs(c)], Lc[:, cs(c)], start=True, stop=True),
                             reads=[Mcb, Lcb], writes=[bkb[5]])
                    if not lastlv:
                        Mn_, Mnb_ = Mx.next()
                        for c in range(4):
                            P.op("pe", lambda e, c=c, Mc=Mc, Lc=Lc: e.matmul(bk[6][:, cs(c)], Lc[:, cs(c)], Mc[:, cs(c)], start=True, stop=True),
                                 reads=[Mcb, Lcb], writes=[bkb[6]])
                    P.op("act", lambda e, Ln_=Ln_: e.activation(out=Ln_[:], in_=bk[5][:], func=AF.Copy), reads=[bkb[5]], writes=[Lnb_])
                    if not lastlv:
                        P.op("dve", lambda e, Mn_=Mn_: e.tensor_copy(Mn_[:], bk[6][:]), reads=[bkb[6]], writes=[Mnb_])
                    Pn_, Pnb_ = Px.next()
                    for c in range(4):
                        P.op("pe", lambda e, c=c, Pc=Pc: e.matmul(bk[7][:, cs(c)], ident, Pc[:, cs(c)], start=True, stop=False),
                             reads=[K.b, Pcb], writes=[bkb[7]])
                        P.op("pe", lambda e, c=c, Pc=Pc, Ln_=Ln_: e.matmul(bk[7][:, cs(c)], Ln_[:, cs(c)], Pc[:, cs(c)], start=False, stop=True),
                             reads=[Lnb_, Pcb], writes=[bkb[7]])
                    if lv % 2 == 0:
                        P.op("dve", lambda e, Pn_=Pn_: e.tensor_copy(Pn_[:], bk[7][:]), reads=[bkb[7]], writes=[Pnb_])
                    else:
                        P.op("act", lambda e, Pn_=Pn_: e.activation(out=Pn_[:], in_=bk[7][:], func=AF.Copy), reads=[bkb[7]], writes=[Pnb_])
                    Lc, Lcb = Ln_, Lnb_
                    if not lastlv:
                        Mc, Mcb = Mn_, Mnb_
                    Pc, Pcb = Pn_, Pnb_
                vb, vbb = vbs.next()
                kbg, kbgb = kbgs.next()
                for c in range(4):
                    n = 4 * g + c
                    P.op("dve", lambda e, c=c, n=n, vb=vb: e.tensor_scalar(
                        out=vb[:, cs(c)], in0=vtok[:, n, :], scalar1=bcol[:, n:n + 1], scalar2=None, op0=ALU.mult),
                        reads=[vtokb[g], bcolb], writes=[vbb])
                    P.op("dve", lambda e, c=c, n=n, kbg=kbg: e.tensor_scalar(
                        out=kbg[:, cs(c)], in0=ktok[:, n, :], scalar1=bg2[:, n:n + 1], scalar2=None, op0=ALU.mult),
                        reads=[ktokb[g], bg2b], writes=[kbgb])
                for c in range(4):
                    P.op("pe", lambda e, c=c, Pc=Pc, vb=vb: e.matmul(bk[2][:, cs(c)], Pc[:, cs(c)], vb[:, cs(c)], start=True, stop=True),
                         reads=[Pcb, vbb], writes=[bkb[2]])
                    P.op("pe", lambda e, c=c, Pc=Pc, kbg=kbg: e.matmul(bk[3][:, cs(c)], kbg[:, cs(c)], Pc[:, cs(c)], start=True, stop=True),
                         reads=[Pcb, kbgb], writes=[bkb[3]])
                P.op("act", lambda e, g=g: e.activation(
                    out=u[:, 4 * g:4 * g + 4, :].rearrange("p a b -> p (a b)"), in_=bk[2][:], func=AF.Copy),
                    reads=[bkb[2], rawfree], writes=[ub[g]])
                P.op("dve", lambda e, g=g: e.tensor_copy(
                    wT[:, 4 * g:4 * g + 4, :].rearrange("p a b -> p (a b)"), bk[3][:]),
                    reads=[bkb[3], tmpTfree], writes=[wTb[g]])
            Sc, Scb = Sst.next()
            P.op("pool", lambda e, Sc=Sc: e.memset(Sc[:], 0.0), writes=[Scb])
            order = range(NCH) if d == 0 else range(NCH - 1, -1, -1)
            for n in order:
                g = n // 4
                P.op("pe", lambda e, n=n, Sc=Sc: e.matmul(bk[4][:, 0:128], wT[:, n, :], Sc[:], start=True, stop=True),
                     reads=[wTb[g], Scb], writes=[bkb[4]])
                vn, vnb = vns.next()
                P.op("dve", lambda e, n=n, vn=vn: e.tensor_tensor(out=vn[:], in0=u[:, n, :], in1=bk[4][:, 0:128], op=ALU.subtract),
                     reads=[ub[g], bkb[4]], writes=[vnb])
                P.op("pe", lambda e, n=n, Sc=Sc: e.matmul(bk[5][:, 0:128], qT[:, cs(n)], Sc[:], start=True, stop=True),
                     reads=[qTb, Scb], writes=[bkb[5]])
                P.op("pe", lambda e, n=n, vn=vn: e.matmul(bk[6][:, 0:128], AT[:, n, :], vn[:], start=True, stop=True),
                     reads=[ATb[g], vnb], writes=[bkb[6]])
                kd, kdb = kds.next()
                P.op("dve", lambda e, n=n, kd=kd: e.tensor_scalar(
                    out=kd[:], in0=ktok[:, n, :], scalar1=kdf[:, n:n + 1], scalar2=None, op0=ALU.mult),
                    reads=[ktokb[g], kdfb], writes=[kdb])
                P.op("pe", lambda e, kd=kd, vn=vn: e.matmul(bk[7][:, 0:128], kd[:], vn[:], start=True, stop=True),
                     reads=[kdb, vnb], writes=[bkb[7]])
                Sn, Snb = Sst.next()
                P.op("dve", lambda e, n=n, Sc=Sc, Sn=Sn: e.scalar_tensor_tensor(
                    out=Sn[:], in0=Sc[:], scalar=glexp[:, n:n + 1], in1=bk[7][:, 0:128], op0=ALU.mult, op1=ALU.add),
                    reads=[Scb, glexpb, bkb[7]], writes=[Snb])
                t1, t1b = t1s.next()
                P.op("act", lambda e, n=n, t1=t1: e.activation(out=t1[:], in_=bk[5][:, 0:128], func=AF.Copy, scale=gexp[:, n:n + 1]),
                     reads=[bkb[5], gexpb], writes=[t1b])
                if d == 0:
                    P.op("dve", lambda e, n=n, t1=t1: e.tensor_tensor(out=oacc[:, n, :], in0=t1[:], in1=bk[6][:, 0:128], op=ALU.add),
                         reads=[t1b, bkb[6]], writes=[oaccb[n]])
                else:
                    t2, t2b = t2s.next()
                    P.op("dve", lambda e, t1=t1, t2=t2: e.tensor_tensor(out=t2[:], in0=t1[:], in1=bk[6][:, 0:128], op=ALU.add),
                         reads=[t1b, bkb[6]], writes=[t2b])
                    P.op("pool", lambda e, n=n, t2=t2: e.tensor_tensor(out=oacc[:, n, :], in0=oacc[:, n, :], in1=t2[:], op=ALU.add),
                         reads=[t2b, oaccb[n]], writes=[oaccb[n]])
                Sc, Scb = Sn, Snb
        ss, ssb = col["ss"]
        for g in range(NG):
            t4, t4b = on4s.next()
            P.op("dve", lambda e, g=g, t4=t4: e.tensor_tensor(
                out=t4[:], in0=oacc[:, 4 * g:4 * g + 4, :].rearrange("p a b -> p (a b)"),
                in1=oacc[:, 4 * g:4 * g + 4, :].rearrange("p a b -> p (a b)"), op=ALU.mult),
                reads=[oaccb[4 * g + c] for c in range(4)], writes=[t4b])
            P.op("dve", lambda e, g=g, t4=t4: e.tensor_reduce(
                out=ss[:, 4 * g:4 * g + 4], in_=t4[:].rearrange("p (a b) -> p a b", a=4), axis=AX.X, op=ALU.add),
                reads=[t4b], writes=[ssb])
        P.op("act", lambda e: e.activation(out=ss[:], in_=ss[:], func=AF.Ln, bias=cm.eps[:], scale=1.0 / 128),
             reads=[ssb, cm.epsb], writes=[ssb])
        P.op("act", lambda e: e.activation(out=ss[:], in_=ss[:], func=AF.Exp, scale=-0.5), reads=[ssb], writes=[ssb])
        P.dma("sp", lambda e, hh=hh: e.dma_start(out=raw[:, 0:S], in_=zs_h[hh * 128:(hh + 1) * 128, :]), writes=[rawb] + ub)
        for g in range(NG):
            t4, t4b = on4s.next()
            for c in range(4):
                n = 4 * g + c
                P.op("dve", lambda e, c=c, n=n, t4=t4: e.tensor_scalar(
                    out=t4[:, cs(c)], in0=oacc[:, n, :], scalar1=ss[:, n:n + 1], scalar2=None, op0=ALU.mult),
                    reads=[oaccb[n], ssb], writes=[t4b])
            b_ = g % 2
            for c in range(4):
                P.op("pe", lambda e, c=c, t4=t4, b_=b_: e.transpose(bk[b_][:, cs(c)], t4[:, cs(c)], ident),
                     reads=[t4b, K.b], writes=[bkb[b_]])
            ot, ob = osts.next()
            P.op("dve", lambda e, g=g, ot=ot, b_=b_: e.scalar_tensor_tensor(
                out=ot[:], in0=bk[b_][:], scalar=og[:, 0:1], in1=raw[:, g * 512:(g + 1) * 512],
                op0=ALU.mult, op1=ALU.mult), reads=[bkb[b_], ogb, rawb], writes=[ob])
            P.dma("sp", lambda e, g=g, ot=ot, hh=hh: e.dma_start(
                out=o_h[hh * 128:(hh + 1) * 128, g * 512:(g + 1) * 512], in_=ot[:]), reads=[ob], key=ob)


def build_even_core_prog(S, T, nh=4):
    cx = Ctx()
    a_halo = cx.dram_in("a_halo", [1024, T + 30])
    cw_t = cx.dram_in("cw_t", [128, 8 * 31])
    cb_t = cx.dram_in("cb_t", [128, 8])
    lng_t = cx.dram_in("lng_t", [128, 8])
    lnb_t = cx.dram_in("lnb_t", [128, 8])
    qkv_h = cx.dram_in("qkv_h", [3, nh, 128, S + 2])
    zs_h = cx.dram_in("zs_h", [nh * 128, S])
    bg_h = cx.dram_in("bg_h", [S, 4 * nh])
    sw_t = cx.dram_in("sw_t", [128, nh * 9])
    og_t = cx.dram_in("og_t", [128, 1])
    consts = cx.dram_in("consts", [128, N_CONST * 128])
    a_fin = cx.dram_out("a_fin", [1024, T])
    o_h = cx.dram_out("o_h", [nh * 128, S])
    cm = Common(cx)
    K = Consts(cx, consts)
    cx.push_scope()
    conv_module_phase(cx, cm, K, a_halo, cw_t, cb_t, lng_t, lnb_t, a_fin, T)
    cx.pop_scope()
    cx.push_scope()
    deltanet_phase(cx, cm, K, qkv_h, zs_h, bg_h, sw_t, og_t, o_h, S, nh)
    cx.pop_scope()
    return cx.finish()


def even_core_inputs(aT2, qkvT2, zsT2, bg2, r, S, T, nh=4):
    a_full = np.concatenate(aT2, axis=1)
    a_pad = np.pad(a_full, ((0, 0), (15, 15)))
    a_halo = np.ascontiguousarray(a_pad[:, r * T:r * T + T + 30])
    qkv = np.concatenate(qkvT2, axis=1).reshape(3, 8, 128, S)[:, nh * r:nh * r + nh]
    qkv_h = np.ascontiguousarray(np.pad(qkv, ((0, 0), (0, 0), (0, 0), (1, 1))))
    zs_h = np.ascontiguousarray(np.concatenate(zsT2, axis=1).reshape(8, 128, S)[nh * r:nh * r + nh].reshape(nh * 128, S))
    bg = np.concatenate(bg2, axis=0).reshape(S, 4, 8)[:, :, nh * r:nh * r + nh]
    bg_h = np.ascontiguousarray(bg.reshape(S, 4 * nh))
    return {"a_halo": a_halo, "qkv_h": qkv_h, "zs_h": zs_h, "bg_h": bg_h}


def even_core_params(conv_w, conv_b, ln_g, ln_b, short_w, onorm_g, r, nh=4):
    cw_t = np.ascontiguousarray(conv_w.reshape(31, 8, 128).transpose(2, 1, 0).reshape(128, 8 * 31))
    sw = short_w.reshape(3, 3, 8, 128)[:, :, nh * r:nh * r + nh]
    sw_t = np.ascontiguousarray(sw.transpose(3, 2, 1, 0).reshape(128, nh * 9))
    return {"cw_t": cw_t, "cb_t": vec_t(conv_b, 8), "lng_t": vec_t(ln_g, 8), "lnb_t": vec_t(ln_b, 8),
            "sw_t": sw_t, "og_t": np.ascontiguousarray(onorm_g.reshape(128, 1)), "consts": make_consts()}


def odd_proj_phase(cx, cm, K, R, xT_in, g_t, w_t, pos_rep, qkvT, T):
    P = cx.P
    TB = R.TB
    nb = TB // 512
    PI = float(np.pi)
    cx.push_scope()
    P.dma("sp", lambda e: e.dma_start(out=R.g[:], in_=g_t[:, :]), writes=[R.gb])
    posi = cx.sb([32, TB], I32, "posi"); posib = P.buf("posi")
    ang = cx.sb([32, TB], F32, "ang"); angb = P.buf("ang")
    cos = cx.sb([32, TB], F32, "cos"); cosb = P.buf("cos")
    sin = cx.sb([32, TB], F32, "sin"); sinb = P.buf("sin")
    rarg = cx.sb([32, TB], F32, "rarg"); rargb = P.buf("rarg")
    kfl = cx.sb([32, TB], F32, "kfl"); kflb = P.buf("kfl")
    kin = cx.sb([32, TB], I32, "kin"); kinb = P.buf("kin")
    rts = Slots(cx, 2, [32, 512], F32, "rot_t")
    invf = K.t[0:32, C_INVF * 128:C_INVF * 128 + 1]
    rotT = K.t[0:32, C_ROT * 128:C_ROT * 128 + 32]
    def _blk(T0):
        norm_to_xn(cx, cm, R, xT_in, R.g, R.gb, T0, TB)
        P.dma("sp", lambda e, T0=T0: e.dma_start(out=posi[:], in_=pos_rep[:, T0:T0 + TB]), writes=[posib])
        P.op("dve", lambda e: e.tensor_copy(ang[:], posi[:]), reads=[posib], writes=[angb])
        P.op("dve", lambda e: e.tensor_scalar(out=ang[:], in0=ang[:], scalar1=invf, scalar2=None, op0=ALU.mult),
             reads=[angb, K.b], writes=[angb])
        C1, C2 = 6.28125, 2.0 * np.pi - 6.28125
        for dst, dstb, shift in ((sin, sinb, 0.0), (cos, cosb, 0.5 * PI)):
            P.op("dve", lambda e, shift=shift: e.tensor_scalar(out=rarg[:], in0=ang[:], scalar1=float(shift), scalar2=None, op0=ALU.add),
                 reads=[angb], writes=[rargb])
            P.op("dve", lambda e: e.tensor_scalar(out=kfl[:], in0=rarg[:], scalar1=float(1.0 / (2 * np.pi)), scalar2=None, op0=ALU.mult),
                 reads=[rargb], writes=[kflb])
            P.op("dve", lambda e: e.tensor_copy(kin[:], kfl[:]), reads=[kflb], writes=[kinb])
            P.op("dve", lambda e: e.tensor_copy(kfl[:], kin[:]), reads=[kinb], writes=[kflb])
            P.op("dve", lambda e: e.scalar_tensor_tensor(out=rarg[:], in0=kfl[:], scalar=-C1, in1=rarg[:], op0=ALU.mult, op1=ALU.add),
                 reads=[kflb, rargb], writes=[rargb])
            P.op("dve", lambda e: e.scalar_tensor_tensor(out=rarg[:], in0=kfl[:], scalar=-C2, in1=rarg[:], op0=ALU.mult, op1=ALU.add),
                 reads=[kflb, rargb], writes=[rargb])
            P.op("dve", lambda e: e.tensor_scalar(out=kfl[:], in0=rarg[:], scalar1=PI, scalar2=-2 * PI, op0=ALU.is_gt, op1=ALU.mult),
                 reads=[rargb], writes=[kflb])
            P.op("dve", lambda e: e.tensor_tensor(out=rarg[:], in0=rarg[:], in1=kfl[:], op=ALU.add), reads=[rargb, kflb], writes=[rargb])
            P.op("dve", lambda e: e.tensor_scalar(out=rarg[:], in0=rarg[:], scalar1=-PI, scalar2=PI, op0=ALU.max, op1=ALU.min),
                 reads=[rargb], writes=[rargb])
            P.op("act", lambda e, dst=dst: e.activation(out=dst[:], in_=rarg[:], func=AF.Sin), reads=[rargb], writes=[dstb])
        for f in range(48):
            bks = [(f % 2) * 2 + h for h in range(nb)]
            proj_tile(cx, cm, R, w_t, f, bks, TB)
            ot, ob = R.ost.next()
            for h in range(nb):
                P.op("act", lambda e, ot=ot, h=h, bks=bks: e.activation(
                    out=ot[:, h * 512:(h + 1) * 512], in_=cm.banks[bks[h]][:], func=AF.Copy),
                    reads=[cm.bankb[bks[h]]], writes=[ob])
            if f < 32:
                for h in range(nb):
                    hs = slice(h * 512, (h + 1) * 512)
                    rb = 4 + h % 2
                    P.op("pe", lambda e, ot=ot, hs=hs, rb=rb: e.matmul(
                        cm.banks[rb][0:32, :], rotT, ot[0:32, hs], start=True, stop=True),
                        reads=[ob, K.b], writes=[cm.bankb[rb]])
                    rt, rtb = rts.next()
                    P.op("dve", lambda e, rt=rt, hs=hs, rb=rb: e.tensor_tensor(
                        out=rt[:], in0=cm.banks[rb][0:32, :], in1=sin[:, hs], op=ALU.mult),
                        reads=[cm.bankb[rb], sinb], writes=[rtb])
                    P.op("dve", lambda e, ot=ot, hs=hs: e.tensor_tensor(
                        out=ot[0:32, hs], in0=ot[0:32, hs], in1=cos[:, hs], op=ALU.mult),
                        reads=[ob, cosb], writes=[ob])
                    P.op("dve", lambda e, ot=ot, rt=rt, hs=hs: e.tensor_tensor(
                        out=ot[0:32, hs], in0=ot[0:32, hs], in1=rt[:], op=ALU.add),
                        reads=[ob, rtb], writes=[ob])
            store_rows(cx, ot, ob, qkvT, f * 128, T0, TB)
    for T0_ in range(0, T, TB):
        _blk(T0_)
    cx.pop_scope()


def attn_phase(cx, cm, K, q_h, k_h, v_h, lam_rep, sg_t, o_h, S, nh, lambda_init):
    P = cx.P
    NKC = S // 128
    NQB = S // 512
    scale = 128.0 ** -0.5
    bk, bkb = cm.banks, cm.bankb
    ident = K.m(C_ID)
    qks = [Slots(cx, 2, [128, 2, S], BF16, "at_q"), Slots(cx, 2, [128, 2, S], BF16, "at_k")]
    vT = cx.sb([128, 2, S], F32, "at_vT"); vTb = P.buf("at_vT")
    vtoks = [(cx.sb([128, NKC, 256], BF16, f"at_vtok{i}"), [P.buf(f"at_vtok{i}_{g}") for g in range(NKC // 2)]) for i in range(2)]
    lamr = cx.sb([128, 4 * 128], F32, "at_lamr"); lamrb = P.buf("lamr")
    sg = cx.sb([128, 2], F32, "at_sg"); sgb = P.buf("sg")
    sm = cx.sb([128, 8], F32, "at_sm"); smb = P.buf("sm")
    mx = cx.sb([128, 4, NQB], F32, "at_mx"); mxb = P.buf("mx")
    negcs = Slots(cx, 2, [128, 2], F32, "at_negc")
    P.dma("sp", lambda e: e.dma_start(out=lamr[:], in_=lam_rep[:, :]), writes=[lamrb])
    P.dma("sp", lambda e: e.dma_start(out=sg[:], in_=sg_t[:, :]), writes=[sgb])
    tmpl = cx.sb([128, 128], F32, "at_tmpl"); tmplb = P.buf("tmpl")
    for i in range(2):
        P.op("dve", lambda e, i=i: e.tensor_tensor(out=tmpl[:], in0=lamr[:, (2 * i) * 128:(2 * i + 1) * 128],
                                                    in1=lamr[:, (2 * i + 1) * 128:(2 * i + 2) * 128], op=ALU.mult),
             reads=[lamrb], writes=[tmplb])
        P.op("dve", lambda e, i=i: e.tensor_reduce(out=sm[:, i:i + 1], in_=tmpl[:], axis=AX.X, op=ALU.add),
             reads=[tmplb], writes=[smb])
    P.op("act", lambda e: e.activation(out=sm[:, 0:2], in_=sm[:, 0:2], func=AF.Exp), reads=[smb], writes=[smb])
    P.op("dve", lambda e: e.tensor_tensor(out=sm[:, 2:3], in0=sm[:, 1:2], in1=sm[:, 0:1], op=ALU.subtract),
         reads=[smb], writes=[smb])
    P.op("dve", lambda e: e.tensor_scalar(out=sm[:, 2:3], in0=sm[:, 2:3], scalar1=-float(lambda_init), scalar2=None, op0=ALU.add),
         reads=[smb], writes=[smb])
    P.op("dve", lambda e: e.tensor_scalar(out=sg[:], in0=sg[:], scalar1=float(1.0 - lambda_init), scalar2=None, op0=ALU.mult),
         reads=[sgb], writes=[sgb])
    sqs = Slots(cx, 2, [128, 512], F32, "at_sq")
    pts = Slots(cx, 4, [128, 512], BF16, "at_pt")
    accs = Slots(cx, 2, [128, 512], F32, "at_acc")
    rinv = Slots(cx, 2, [128, 512], F32, "at_rinv")
    om = [[Slots(cx, 1, [128, 512], F32, f"at_om{m}{h}") for h in range(2)] for m in range(2)]
    od = [Slots(cx, 2, [128, 512], F32, f"at_od{h}") for h in range(2)]
    rss = Slots(cx, 2, [128, 512], F32, "at_rs")
    osts = Slots(cx, 4, [128, 512], F32, "at_ost")
    B_RS, B_LN = 2, 5

    def load_qk(hh):
        qt, qtb = qks[0].next()
        kt, ktb = qks[1].next()
        P.dma("pool", lambda e: e.dma_start(out=qt[:], in_=q_h[hh].rearrange("m p s -> p m s")), writes=[qtb])
        P.dma("pool", lambda e: e.dma_start(out=kt[:], in_=k_h[hh].rearrange("m p s -> p m s")), writes=[ktb])
        return qt, qtb, kt, ktb

    def load_v(hh):
        P.dma("sp", lambda e: e.dma_start(out=vT[:], in_=v_h[hh].rearrange("m p s -> p m s")), writes=[vTb])

    B_T, B_M = 2, 5

    def prep_units(h, bufs):
        qt, qtb, kt, ktb = bufs
        vtk, vtkb = vtoks[h % 2]
        ng, ngb = negcs.next()
        units = []

        def u_tr(kc2):
            for j in range(2):
                kc = 2 * kc2 + j
                for half in range(2):
                    P.op("pe", lambda e, j=j, half=half, kc=kc: e.transpose(
                        bk[B_T][:, (j * 2 + half) * 128:(j * 2 + half + 1) * 128], vT[:, half, kc * 128:(kc + 1) * 128], ident),
                        reads=[vTb, K.b], writes=[bkb[B_T]])
            P.op("dve", lambda e: e.tensor_copy(
                vtk[:, 2 * kc2:2 * kc2 + 2, :].rearrange("p a b -> p (a b)"), bk[B_T][:]),
                reads=[bkb[B_T]], writes=[vtkb[kc2]])

        def u_mx(m, which, blk):
            src_, srcb = ((qt, qtb), (kt, ktb))[which]
            st, sb_ = sqs.next()
            P.op("act", lambda e: e.activation(
                out=st[:], in_=src_[:, m, blk * 512:(blk + 1) * 512], func=AF.Square), reads=[srcb], writes=[sb_])
            P.op("pe", lambda e: e.matmul(bk[B_M][:], cm.ones_f[:], st[:], start=True, stop=True),
                 reads=[sb_, cm.ones_fb], writes=[bkb[B_M]])
            P.op("dve", lambda e: e.tensor_reduce(
                out=mx[:, m * 2 + which, blk:blk + 1], in_=bk[B_M][:], axis=AX.X, op=ALU.max),
                reads=[bkb[B_M]], writes=[mxb])

        def u_fin():
            P.op("dve", lambda e: e.tensor_reduce(out=sm[:, 4:8], in_=mx[:], axis=AX.X, op=ALU.max), reads=[mxb], writes=[smb])
            for m in range(2):
                P.op("dve", lambda e, m=m: e.tensor_tensor(out=ng[:, m:m + 1], in0=sm[:, 4 + 2 * m:5 + 2 * m],
                                                        in1=sm[:, 5 + 2 * m:6 + 2 * m], op=ALU.mult),
                     reads=[smb], writes=[ngb])
            P.op("act", lambda e: e.activation(out=ng[:], in_=ng[:], func=AF.Ln), reads=[ngb], writes=[ngb])
            P.op("act", lambda e: e.activation(out=ng[:], in_=ng[:], func=AF.Exp, scale=0.5), reads=[ngb], writes=[ngb])
            P.op("dve", lambda e: e.tensor_scalar(out=ng[:], in0=ng[:], scalar1=-scale, scalar2=None, op0=ALU.mult),
                 reads=[ngb], writes=[ngb])

        for kc2 in range(NKC // 2):
            units.append(lambda kc2=kc2: u_tr(kc2))
        if h + 1 < nh:
            units.append(lambda: load_v(h + 1))
        for m in range(2):
            for which in range(2):
                for blk in range(NQB):
                    units.append(lambda m=m, which=which, blk=blk: u_mx(m, which, blk))
        units.append(u_fin)
        return units, (qt, qtb, kt, ktb, vtk, vtkb, ng, ngb)

    cur = load_qk(0)
    load_v(0)
    units, hd = prep_units(0, cur)
    for u_ in units:
        u_()
    for hh in range(nh):
        qb_, qbb, kb_, kbb, vtok, vtokb, ng, ngb = hd
        pending = []
        if hh + 1 < nh:
            pending, hd = prep_units(hh + 1, load_qk(hh + 1))
        steps = [(qb, m, kc) for qb in range(NQB) for m in range(2) for kc in range(NKC)]

        def emit_qk(i, qb_=qb_, kb_=kb_, ng=ng):
            qb, m, kc = steps[i]
            qs = slice(qb * 512, (qb + 1) * 512)
            sb_k = 6 + i % 2
            P.op("pe", lambda e: e.matmul(
                bk[sb_k][:], kb_[:, m, kc * 128:(kc + 1) * 128], qb_[:, m, qs], start=True, stop=True),
                reads=[kbb, qbb], writes=[bkb[sb_k]])
            pt, ptb = pts.next()
            P.op("act", lambda e: e.activation(
                out=pt[:], in_=bk[sb_k][:], func=AF.Exp, bias=ng[:, m:m + 1], scale=scale),
                reads=[bkb[sb_k], ngb], writes=[ptb])
            return pt, ptb

        state = {"omt": [[None, None], [None, None]], "acc": None}

        def emit_pv(i, pt, ptb, vtok=vtok, vtokb=vtokb):
            qb, m, kc = steps[i]
            a_lo, a_hi = 3 * m, 3 * m + 1
            st_, sp_ = (kc == 0), (kc == NKC - 1)
            P.op("pe", lambda e: e.matmul(bk[a_lo][:], vtok[:, kc, 0:128], pt[:], start=st_, stop=sp_),
                 reads=[vtokb[kc // 2], ptb], writes=[bkb[a_lo]])
            P.op("pe", lambda e: e.matmul(bk[a_hi][:], vtok[:, kc, 128:256], pt[:], start=st_, stop=sp_),
                 reads=[vtokb[kc // 2], ptb], writes=[bkb[a_hi]])
            if kc == 0:
                state["acc"] = accs.next()
                ac, acb = state["acc"]
                P.op("dve", lambda e: e.tensor_copy(ac[:], pt[:]), reads=[ptb], writes=[acb])
            else:
                ac, acb = state["acc"]
                P.op("dve", lambda e: e.tensor_tensor(out=ac[:], in0=ac[:], in1=pt[:], op=ALU.add),
                     reads=[ptb, acb], writes=[acb])

        def emit_post(i, acc, hh=hh):
            qb, m, kc = steps[i]
            qs = slice(qb * 512, (qb + 1) * 512)
            a_lo, a_hi = 3 * m, 3 * m + 1
            ac, acb = acc
            omt = state["omt"]
            P.op("pe", lambda e: e.matmul(bk[B_RS][:], cm.ones_f[:], ac[:], start=True, stop=True),
                 reads=[cm.ones_fb, acb], writes=[bkb[B_RS]])
            ri, rib = rinv.next()
            P.op("dve", lambda e: e.reciprocal(ri[:], bk[B_RS][:]), reads=[bkb[B_RS]], writes=[rib])
            for half, ab in enumerate((a_lo, a_hi)):
                t_, tb_ = om[m][half].next()
                omt[m][half] = (t_, tb_)
                P.op("dve", lambda e, t_=t_, ab=ab: e.tensor_tensor(out=t_[:], in0=bk[ab][:], in1=ri[:], op=ALU.mult),
                     reads=[bkb[ab], rib], writes=[tb_])
            if m == 0:
                return
            ods = []
            for half in range(2):
                t_, tb_ = od[half].next()
                ods.append((t_, tb_))
                P.op("dve", lambda e, t_=t_, half=half: e.scalar_tensor_tensor(
                    out=t_[:], in0=omt[1][half][0][:], scalar=sm[:, 2:3], in1=omt[0][half][0][:],
                    op0=ALU.mult, op1=ALU.add), reads=[omt[1][half][1], omt[0][half][1], smb], writes=[tb_])
            for half in range(2):
                st, sb_ = sqs.next()
                P.op("act", lambda e, st=st, half=half: e.activation(out=st[:], in_=ods[half][0][:], func=AF.Square),
                     reads=[ods[half][1]], writes=[sb_])
                P.op("pe", lambda e, st=st, half=half: e.matmul(bk[B_LN][:], cm.ones_f[:], st[:], start=(half == 0), stop=(half == 1)),
                     reads=[sb_, cm.ones_fb], writes=[bkb[B_LN]])
            rs, rsb = rss.next()
            P.op("act", lambda e: e.activation(out=rs[:], in_=bk[B_LN][:], func=AF.Ln, bias=cm.eps[:], scale=1.0 / 256),
                 reads=[bkb[B_LN], cm.epsb], writes=[rsb])
            P.op("act", lambda e: e.activation(out=rs[:], in_=rs[:], func=AF.Exp, scale=-0.5), reads=[rsb], writes=[rsb])
            for half in range(2):
                ot, ob = osts.next()
                P.op("dve", lambda e, ot=ot, half=half: e.scalar_tensor_tensor(
                    out=ot[:], in0=ods[half][0][:], scalar=sg[:, half:half + 1], in1=rs[:], op0=ALU.mult, op1=ALU.mult),
                    reads=[ods[half][1], sgb, rsb], writes=[ob])
                P.dma("sp", lambda e, ot=ot, half=half: e.dma_start(
                    out=o_h[hh * 256 + half * 128:hh * 256 + (half + 1) * 128, qs], in_=ot[:]), reads=[ob], key=ob)

        pend = emit_qk(0)
        deferred = None
        for i in range(len(steps)):
            nxt = emit_qk(i + 1) if i + 1 < len(steps) else None
            emit_pv(i, *pend)
            if deferred is not None:
                emit_post(*deferred)
                deferred = None
            if steps[i][2] == NKC - 1:
                deferred = (i, state["acc"])
            if pending and i % 8 == 7:
                pending.pop(0)()
            pend = nxt
        emit_post(*deferred)
        for u_ in pending:
            u_()


def build_odd_proj_prog(T, TB=1024):
    TB = min(TB, T)
    cx = Ctx()
    xT_in = cx.dram_in("xT_in", [D_MODEL, T])
    g_t = cx.dram_in("g_t", [128, NC_D])
    w_t = cx.dram_in("w_t", [48, 128, NC_D * 128])
    pos_rep = cx.dram_in("pos_rep", [32, T], I32)
    consts = cx.dram_in("consts", [128, N_CONST * 128])
    qkvT = cx.dram_out("qkvT", [6144, T])
    cm = Common(cx)
    K = Consts(cx, consts)
    R = FFNRes(cx, cm, TB)
    odd_proj_phase(cx, cm, K, R, xT_in, g_t, w_t, pos_rep, qkvT, T)
    return cx.finish()


def build_attn_prog(S, nh, lambda_init):
    cx = Ctx()
    q_h = cx.dram_in("q_h", [nh, 2, 128, S])
    k_h = cx.dram_in("k_h", [nh, 2, 128, S])
    v_h = cx.dram_in("v_h", [nh, 2, 128, S])
    lam_rep = cx.dram_in("lam_rep", [128, 4 * 128])
    sg_t = cx.dram_in("sg_t", [128, 2])
    consts = cx.dram_in("consts", [128, N_CONST * 128])
    o_h = cx.dram_out("o_h", [nh * 256, S])
    cm = Common(cx)
    K = Consts(cx, consts)
    attn_phase(cx, cm, K, q_h, k_h, v_h, lam_rep, sg_t, o_h, S, nh, lambda_init)
    return cx.finish()


def attn_inputs(qkvT2, r, S, nh=4):
    full = np.concatenate(qkvT2, axis=1)
    q = full[0:2048].reshape(8, 2, 128, S)[nh * r:nh * r + nh]
    k = full[2048:4096].reshape(8, 2, 128, S)[nh * r:nh * r + nh]
    v = full[4096:6144].reshape(8, 2, 128, S)[nh * r:nh * r + nh]
    return {"q_h": np.ascontiguousarray(q), "k_h": np.ascontiguousarray(k), "v_h": np.ascontiguousarray(v)}


def attn_params(lq1, lk1, lq2, lk2, subln_g):
    lam_rep = np.ascontiguousarray(np.broadcast_to(np.concatenate([lq1, lk1, lq2, lk2])[None, :], (128, 512)))
    return {"lam_rep": lam_rep, "sg_t": vec_t(subln_g, 2), "consts": make_consts()}


T_CORE = SEQ // 2
TBLK = 1024


def _ffn_ins(cx, tag):
    return (cx.dram_in(f"g_{tag}", [128, NC_D]), cx.dram_in(f"wg_{tag}", [NC_F, 128, NC_D * 128]),
            cx.dram_in(f"wu_{tag}", [NC_F, 128, NC_D * 128]), cx.dram_in(f"wd_{tag}", [NC_D, 128, NC_F * 128]))


def build_L1(T):
    cx = Ctx()
    xT_in = cx.dram_in("xT_in", [D_MODEL, T])
    f1 = _ffn_ins(cx, "f1")
    gm = cx.dram_in("g_mix", [128, NC_D])
    w_main = cx.dram_in("w_main", [48, 128, NC_D * 128])
    w_ba = cx.dram_in("w_ba", [128, 16 * 32])
    dtb = cx.dram_in("dtb_rep", [128, 16])
    alog = cx.dram_in("alog_rep", [128, 16])
    x1T = cx.dram_out("x1T", [D_MODEL, T])
    aT = cx.dram_out("aT", [1024, T])
    qkvT = cx.dram_out("qkvT", [3072, T])
    zsT = cx.dram_out("zsT", [1024, T])
    bg = cx.dram_out("bg_tok", [T, 32])
    cm = Common(cx)
    R = FFNRes(cx, cm, min(TBLK, T))
    ffn_phase(cx, cm, R, xT_in, x1T, f1[0], f1[1], f1[2], f1[3], T)
    even_proj_phase(cx, cm, R, x1T, gm, w_main, w_ba, dtb, alog, aT, qkvT, zsT, bg, T)
    return cx.finish()


def build_L3(T):
    cx = Ctx()
    catT = cx.dram_in("catT", [D_MODEL, T])
    x1T = cx.dram_in("xT_in", [D_MODEL, T])
    w_out = cx.dram_in("w_out", [NC_D, 128, NC_D * 128])
    f2 = _ffn_ins(cx, "f2")
    f1 = _ffn_ins(cx, "f1")
    gm = cx.dram_in("g_mix", [128, NC_D])
    w_qkv = cx.dram_in("w_qkv", [48, 128, NC_D * 128])
    pos_rep = cx.dram_in("pos_rep", [32, T], I32)
    consts = cx.dram_in("consts", [128, N_CONST * 128])
    xa = cx.dram_tmp("xa", [D_MODEL, T])
    xb = cx.dram_tmp("xb", [D_MODEL, T])
    xc = cx.dram_out("xT_out", [D_MODEL, T])
    qkvT = cx.dram_out("qkvT", [6144, T])
    cm = Common(cx)
    K = Consts(cx, consts)
    R = FFNRes(cx, cm, min(TBLK, T))
    outproj_phase(cx, cm, R, catT, w_out, x1T, xa, T)
    ffn_phase(cx, cm, R, xa, xb, f2[0], f2[1], f2[2], f2[3], T)
    ffn_phase(cx, cm, R, xb, xc, f1[0], f1[1], f1[2], f1[3], T)
    odd_proj_phase(cx, cm, K, R, xc, gm, w_qkv, pos_rep, qkvT, T)
    return cx.finish()


def build_L5(T):
    cx = Ctx()
    catT = cx.dram_in("catT", [D_MODEL, T])
    xT_in = cx.dram_in("xT_in", [D_MODEL, T])
    w_o = cx.dram_in("w_out", [NC_D, 128, NC_D * 128])
    f2 = _ffn_ins(cx, "f2")
    gf = cx.dram_in("g_fin", [128, NC_D])
    xa = cx.dram_tmp("xa", [D_MODEL, T])
    xb = cx.dram_tmp("xb", [D_MODEL, T])
    outT = cx.dram_out("outT", [D_MODEL, T])
    cm = Common(cx)
    R = FFNRes(cx, cm, min(TBLK, T))
    outproj_phase(cx, cm, R, catT, w_o, xT_in, xa, T)
    ffn_phase(cx, cm, R, xa, xb, f2[0], f2[1], f2[2], f2[3], T)
    final_norm_phase(cx, cm, R, xb, gf, outT, T)
    return cx.finish()


_PROGS = {}


def _prog(name, fn, *a):
    k = (name,) + a
    if k not in _PROGS:
        _PROGS[k] = fn(*a)
    return _PROGS[k]


def _ffn_w(tag, g, wg, wu, wd):
    return {f"g_{tag}": vec_t(g, NC_D), f"wg_{tag}": tile_w_in(wg, NC_D, NC_F),
            f"wu_{tag}": tile_w_in(wu, NC_D, NC_F), f"wd_{tag}": tile_w_in(wd, NC_F, NC_D)}


def _run(nc, in_maps):
    res = run_bass_kernel_spmd(nc, in_maps, core_ids=list(range(NCORES)))
    return res.results


def kernel(x, positions, norm_ffn1, ffn1_wg, ffn1_wu, ffn1_wd, norm_mix,
           norm_ffn2, ffn2_wg, ffn2_wu, ffn2_wd,
           ev_w_in, ev_conv_w, ev_conv_b, ev_ln_g, ev_ln_b, ev_short_w,
           ev_a_log, ev_dt_bias, ev_onorm_g, ev_w_out,
           od_w_qkv, od_lq1, od_lk1, od_lq2, od_lk2, od_subln_g, od_w_o,
           final_norm):
    f32 = lambda a: np.ascontiguousarray(np.asarray(a, dtype=np.float32))
    x = f32(x)
    positions = np.asarray(positions, dtype=np.int32)
    S, T = SEQ, T_CORE
    cores = [(c // 2, c % 2) for c in range(NCORES)]
    w_in = f32(ev_w_in[0])
    sh = _ffn_w("f1", f32(norm_ffn1[0]), f32(ffn1_wg[0]), f32(ffn1_wu[0]), f32(ffn1_wd[0]))
    sh.update({"g_mix": vec_t(f32(norm_mix[0]), NC_D), "w_main": tile_w_in(w_in[:, :6144], NC_D, 48),
               "w_ba": np.ascontiguousarray(w_in[:, 6144:6176].reshape(16, 128, 32).transpose(1, 0, 2).reshape(128, 512)),
               "dtb_rep": np.ascontiguousarray(np.broadcast_to(f32(ev_dt_bias[0]).reshape(1, 16), (128, 16))),
               "alog_rep": np.ascontiguousarray(np.broadcast_to(f32(ev_a_log[0]).reshape(1, 16), (128, 16)))})
    ins = [dict(sh, xT_in=np.ascontiguousarray(x[b, r * T:(r + 1) * T].T)) for (b, r) in cores]
    r1 = _run(_prog("L1", build_L1, T), ins)
    del ins, sh
    ins = []
    for c, (b, r) in enumerate(cores):
        pair = [r1[2 * b], r1[2 * b + 1]]
        m = even_core_inputs([p["aT"] for p in pair], [p["qkvT"] for p in pair], [p["zsT"] for p in pair],
                             [p["bg_tok"] for p in pair], r, S, T)
        m.update(even_core_params(f32(ev_conv_w[0]), f32(ev_conv_b[0]), f32(ev_ln_g[0]), f32(ev_ln_b[0]),
                                  f32(ev_short_w[0]), f32(ev_onorm_g[0]), r))
        ins.append(m)
    r2 = _run(_prog("L2", build_even_core_prog, S, T), ins)
    sh = _ffn_w("f2", f32(norm_ffn2[0]), f32(ffn2_wg[0]), f32(ffn2_wu[0]), f32(ffn2_wd[0]))
    sh.update(_ffn_w("f1", f32(norm_ffn1[1]), f32(ffn1_wg[1]), f32(ffn1_wu[1]), f32(ffn1_wd[1])))
    sh.update({"w_out": tile_w_in(f32(ev_w_out[0]), NC_D, NC_D), "g_mix": vec_t(f32(norm_mix[1]), NC_D),
               "w_qkv": tile_w_in(f32(od_w_qkv[0]), NC_D, 48), "consts": make_consts()})
    ins = []
    for c, (b, r) in enumerate(cores):
        o_pair = np.concatenate([r2[2 * b]["o_h"][:, r * T:(r + 1) * T], r2[2 * b + 1]["o_h"][:, r * T:(r + 1) * T]], axis=0)
        catT = np.ascontiguousarray(np.concatenate([r2[c]["a_fin"], o_pair], axis=0))
        ins.append(dict(sh, catT=catT, xT_in=r1[c]["x1T"],
                        pos_rep=np.ascontiguousarray(np.broadcast_to(positions[b, r * T:(r + 1) * T][None, :], (32, T)))))
    r3 = _run(_prog("L3", build_L3, T), ins)
    del ins, sh, r1, r2
    lam_init = 0.8 - 0.6 * float(np.exp(-0.3 * 1))
    ins = []
    for c, (b, r) in enumerate(cores):
        m = attn_inputs([r3[2 * b]["qkvT"], r3[2 * b + 1]["qkvT"]], r, S)
        m.update(attn_params(f32(od_lq1[0]), f32(od_lk1[0]), f32(od_lq2[0]), f32(od_lk2[0]), f32(od_subln_g[0])))
        ins.append(m)
    r4 = _run(_prog("L4", build_attn_prog, S, 4, lam_init), ins)
    sh = _ffn_w("f2", f32(norm_ffn2[1]), f32(ffn2_wg[1]), f32(ffn2_wu[1]), f32(ffn2_wd[1]))
    sh.update({"w_out": tile_w_in(f32(od_w_o[0]), NC_D, NC_D), "g_fin": vec_t(f32(final_norm), NC_D)})
    ins = []
    for c, (b, r) in enumerate(cores):
        catT = np.ascontiguousarray(np.concatenate([r4[2 * b]["o_h"][:, r * T:(r + 1) * T],
                                                    r4[2 * b + 1]["o_h"][:, r * T:(r + 1) * T]], axis=0))
        ins.append(dict(sh, catT=catT, xT_in=r3[c]["xT_out"]))
    r5 = _run(_prog("L5", build_L5, T), ins)
    out = np.empty((BATCH, SEQ, D_MODEL), np.float32)
    for c, (b, r) in enumerate(cores):
        out[b, r * T:(r + 1) * T, :] = r5[c]["outT"].T
    return out
```
